# Optimizing a Trainium2 kernel written in Bass

```python
import math
import jax, jax.numpy as jnp
from jax import lax
import numpy as np

D_MODEL = 1024
BATCH = 8
SEQ = 4096
DEPTH = 2

N_A = DEPTH // 2
N_B = DEPTH - N_A

D_FF = 2816

D_RNN = 1280
N_GATE_BLOCKS = 5
GATE_BW = D_RNN // N_GATE_BLOCKS
CONV_WIDTH = 4
LRU_C = 8.0

N_HEADS = 16
HEAD_DIM = D_MODEL // N_HEADS
DILATION_PAIRS = ((128, 1), (512, 4), (2048, 16))
N_GROUPS = len(DILATION_PAIRS)

DEEPNORM_ALPHA = float((2 * DEPTH) ** 0.25)
DEEPNORM_BETA = float((8 * DEPTH) ** -0.25)

LN_EPS = 1e-5
NEG_INF = -1e30

kernel_name = "yoco_rglru_dilated_alibi_macaron_deepnorm"


def layer_norm(x, g, b):
    xf = x.astype(jnp.float32)
    mu = jnp.mean(xf, axis=-1, keepdims=True)
    xc = xf - mu
    var = jnp.mean(xc * xc, axis=-1, keepdims=True)
    y = xc * lax.rsqrt(var + LN_EPS) * g.astype(jnp.float32) + b.astype(jnp.float32)
    return y.astype(x.dtype)


def swiglu(x, w_in, w_out):
    gate, up = jnp.split(x @ w_in, 2, axis=-1)
    return (jax.nn.silu(gate) * up) @ w_out


def causal_depthwise_conv(u, w, b):
    r = u.shape[-1]
    rhs = w.astype(u.dtype)[:, None, :]
    y = lax.conv_general_dilated(
        u, rhs, window_strides=(1,), padding=[(CONV_WIDTH - 1, 0)],
        dimension_numbers=("NWC", "WIO", "NWC"), feature_group_count=r)
    return y + b.astype(u.dtype)


def rg_lru(u, gate_w, gate_b, lam):
    uf = u.astype(jnp.float32)
    bsz, s, r = uf.shape
    ublk = uf.reshape(bsz, s, N_GATE_BLOCKS, GATE_BW)
    gates = jnp.einsum("bsnc,gncd->gbsnd", ublk, gate_w.astype(jnp.float32)).reshape(2, bsz, s, r)
    gates = gates + gate_b.astype(jnp.float32)[:, None, None, :]
    rec_gate = jax.nn.sigmoid(gates[0])
    in_gate = jax.nn.sigmoid(gates[1])
    log_a = -LRU_C * rec_gate * jax.nn.softplus(-lam.astype(jnp.float32))
    a = jnp.exp(log_a)
    b = jnp.sqrt(-jnp.expm1(2.0 * log_a)) * (in_gate * uf)

    def combine(left, right):
        a1, b1 = left
        a2, b2 = right
        return a1 * a2, a2 * b1 + b2

    _, h = lax.associative_scan(combine, (a, b), axis=1)
    return h


def rglru_block(x, w_in, conv_w, conv_b, gate_w, gate_b, lam, w_out):
    y_br, u_br = jnp.split(x @ w_in, 2, axis=-1)
    y = jax.nn.gelu(y_br)
    u = causal_depthwise_conv(u_br, conv_w, conv_b)
    h = rg_lru(u, gate_w, gate_b, lam)
    return (y * h.astype(y.dtype)) @ w_out


def alibi_slopes(n):
    return jnp.exp2(-8.0 * (jnp.arange(n, dtype=jnp.float32) + 1.0) / n)


def dilated_group(q, k, v, slopes, window, dilation):
    bsz, s, h, dh = q.shape
    reach = window // dilation
    blk = reach
    unit = dilation * blk
    s_pad = -(-s // unit) * unit
    length = s_pad // dilation
    nb = length // blk

    def to_blocks(t):
        t = jnp.pad(t, ((0, 0), (0, s_pad - s), (0, 0), (0, 0)))
        t = t.reshape(bsz, length, dilation, h, dh).transpose(0, 2, 1, 3, 4)
        return t.reshape(bsz, dilation, nb, blk, h, dh)

    def with_prev(t):
        prev = jnp.pad(t, ((0, 0), (0, 0), (1, 0), (0, 0), (0, 0), (0, 0)))[:, :, :-1]
        return jnp.concatenate([prev, t], axis=3)

    qb = to_blocks(q)
    kw = with_prev(to_blocks(k))
    vw = with_prev(to_blocks(v))

    scores = jnp.einsum("bpnqhd,bpnkhd->bpnhqk", qb, kw) * (1.0 / math.sqrt(dh))
    q_idx = jnp.arange(blk)[:, None] + blk
    k_idx = jnp.arange(2 * blk)[None, :]
    dist = q_idx - k_idx
    key_abs = jnp.arange(nb)[:, None] * blk - blk + k_idx
    valid = ((dist >= 0) & (dist <= reach))[None] & (key_abs >= 0)[:, None, :]
    bias = -slopes[:, None, None] * (dist * dilation).astype(jnp.float32)[None]
    scores = jnp.where(valid[:, None], scores + bias, NEG_INF)

    m = jnp.max(scores, axis=-1, keepdims=True)
    p = jnp.exp(scores - m)
    denom = jnp.sum(p, axis=-1, keepdims=True)
    lse = (m + jnp.log(denom))[..., 0]
    out = jnp.einsum("bpnhqk,bpnkhd->bpnqhd", p / denom, vw)

    out = out.reshape(bsz, dilation, length, h, dh).transpose(0, 2, 1, 3, 4).reshape(bsz, s_pad, h, dh)[:, :s]
    lse = lse.transpose(0, 1, 2, 4, 3).reshape(bsz, dilation, length, h).transpose(0, 2, 1, 3).reshape(bsz, s_pad, h)[:, :s]
    return out, lse


def dilated_attention(x, k, v, w_q, w_o):
    bsz, s, _ = x.shape
    q = (x @ w_q).reshape(bsz, s, N_GROUPS, N_HEADS, HEAD_DIM).astype(jnp.float32)
    slopes = alibi_slopes(N_HEADS)
    outs, lses = [], []
    for g, (window, dilation) in enumerate(DILATION_PAIRS):
        o, l = dilated_group(q[:, :, g], k, v, slopes, window, dilation)
        outs.append(o)
        lses.append(l)
    wts = jax.nn.softmax(jnp.stack(lses, axis=0), axis=0)
    out = jnp.sum(wts[..., None] * jnp.stack(outs, axis=0), axis=0)
    return out.reshape(bsz, s, N_HEADS * HEAD_DIM).astype(x.dtype) @ w_o


def setup_inputs(seed: int = 0) -> dict:
    key = jax.random.key(seed)
    ks = jax.random.split(key, 20)
    f32 = jnp.float32
    nrm = lambda k, shape, scale: jax.random.normal(k, shape, f32) * scale

    x = jax.random.normal(ks[0], (BATCH, SEQ, D_MODEL), f32)
    ln_g = 1.0 + nrm(ks[1], (DEPTH, 3, D_MODEL), 0.02)
    ln_b = nrm(ks[2], (DEPTH, 3, D_MODEL), 0.02)
    ffn_w_in = nrm(ks[3], (DEPTH, 2, D_MODEL, 2 * D_FF), D_MODEL ** -0.5)
    ffn_w_out = nrm(ks[4], (DEPTH, 2, D_FF, D_MODEL), D_FF ** -0.5 * DEEPNORM_BETA)

    rg_w_in = nrm(ks[5], (N_A, D_MODEL, 2 * D_RNN), D_MODEL ** -0.5)
    rg_conv_w = nrm(ks[6], (N_A, CONV_WIDTH, D_RNN), CONV_WIDTH ** -0.5)
    rg_conv_b = nrm(ks[7], (N_A, D_RNN), 0.02)
    rg_gate_w = nrm(ks[8], (N_A, 2, N_GATE_BLOCKS, GATE_BW, GATE_BW), GATE_BW ** -0.5)
    rg_gate_b = nrm(ks[9], (N_A, 2, D_RNN), 0.02)
    a0 = jax.random.uniform(ks[10], (N_A, D_RNN), f32, 0.9, 0.999)
    rg_lam = jnp.log(a0) - jnp.log1p(-a0)
    rg_w_out = nrm(ks[11], (N_A, D_RNN, D_MODEL), D_RNN ** -0.5 * DEEPNORM_BETA)

    kv_w = nrm(ks[12], (D_MODEL, 2 * N_HEADS * HEAD_DIM), D_MODEL ** -0.5)
    attn_w_q = nrm(ks[13], (N_B, D_MODEL, N_GROUPS * N_HEADS * HEAD_DIM), D_MODEL ** -0.5)
    attn_w_o = nrm(ks[14], (N_B, N_HEADS * HEAD_DIM, D_MODEL), (N_HEADS * HEAD_DIM) ** -0.5 * DEEPNORM_BETA)
    return {
        "x": x, "ln_g": ln_g, "ln_b": ln_b, "ffn_w_in": ffn_w_in, "ffn_w_out": ffn_w_out,
        "rg_w_in": rg_w_in, "rg_conv_w": rg_conv_w, "rg_conv_b": rg_conv_b,
        "rg_gate_w": rg_gate_w, "rg_gate_b": rg_gate_b, "rg_lam": rg_lam, "rg_w_out": rg_w_out,
        "kv_w": kv_w, "attn_w_q": attn_w_q, "attn_w_o": attn_w_o,
    }


def reference(x, ln_g, ln_b, ffn_w_in, ffn_w_out, rg_w_in, rg_conv_w, rg_conv_b,
              rg_gate_w, rg_gate_b, rg_lam, rg_w_out, kv_w, attn_w_q, attn_w_o):
    bsz, s, _ = x.shape
    k_shared = None
    v_shared = None
    for layer in range(DEPTH):
        x = layer_norm(DEEPNORM_ALPHA * x + 0.5 * swiglu(x, ffn_w_in[layer, 0], ffn_w_out[layer, 0]),
                       ln_g[layer, 0], ln_b[layer, 0])
        if layer < N_A:
            j = layer
            mix = rglru_block(x, rg_w_in[j], rg_conv_w[j], rg_conv_b[j], rg_gate_w[j],
                              rg_gate_b[j], rg_lam[j], rg_w_out[j])
        else:
            j = layer - N_A
            mix = dilated_attention(x, k_shared, v_shared, attn_w_q[j], attn_w_o[j])
        x = layer_norm(DEEPNORM_ALPHA * x + mix, ln_g[layer, 1], ln_b[layer, 1])
        x = layer_norm(DEEPNORM_ALPHA * x + 0.5 * swiglu(x, ffn_w_in[layer, 1], ffn_w_out[layer, 1]),
                       ln_g[layer, 2], ln_b[layer, 2])
        if layer == N_A - 1:
            kv = (x @ kv_w).reshape(bsz, s, 2, N_HEADS, HEAD_DIM).astype(jnp.float32)
            k_shared = kv[:, :, 0]
            v_shared = kv[:, :, 1]
    return x
```

```python
from contextlib import ExitStack
import numpy as np
import concourse.bass as bass
import concourse.mybir as mybir

F32 = mybir.dt.float32
BF16 = mybir.dt.bfloat16
AF = mybir.ActivationFunctionType
ALU = mybir.AluOpType
AX = mybir.AxisListType

ENGS = ("pe", "act", "dve", "pool", "sp")


class DSem:
    def __init__(self, name):
        self.name = name
        self.count = 0
        self.h = None


class Op:
    __slots__ = ("eng", "idx", "fn", "waits", "signal", "dsem", "dval", "val")

    def __init__(self, eng, idx, fn, dsem):
        self.eng = eng
        self.idx = idx
        self.fn = fn
        self.waits = []
        self.signal = False
        self.dsem = dsem
        self.dval = 0
        self.val = 0


class Buf:
    def __init__(self, name):
        self.name = name
        self.last_w = None
        self.writers = {}
        self.readers = {}


class Prog:
    def __init__(self, nc):
        self.nc = nc
        self.ops = {e: [] for e in ENGS}
        self.waited = {e: {} for e in ENGS}
        self.dsems = []
        self.stack = ExitStack()

    def sbuf(self, name, shape, dt):
        return self.stack.enter_context(self.nc.sbuf_tensor("sb_" + name, list(shape), dt))

    def psum(self, name, shape, dt):
        return self.stack.enter_context(self.nc.psum_tensor("ps_" + name, list(shape), dt))

    def dsem(self, name):
        d = DSem(name)
        self.dsems.append(d)
        return d

    def _dep(self, op, other):
        if other is None or other is op or other.fn is None:
            return
        if other.dsem is None and other.eng == "pe" and op.eng == "pe" and op.dsem is None:
            return
        if other.dsem is not None:
            key, ordv = other.dsem, other.dval
        else:
            key, ordv = other.eng, other.idx
        w = self.waited[op.eng]
        if w.get(key, -1) >= ordv:
            return
        w[key] = ordv
        other.signal = True
        op.waits.append(other)

    def op(self, eng, fn, reads=(), writes=(), dsem=None):
        o = Op(eng, len(self.ops[eng]), fn, dsem)
        if dsem is not None:
            dsem.count += 16
            o.dval = dsem.count
        for b in reads:
            for r in b.writers.values():
                self._dep(o, r)
        for b in writes:
            for r in b.writers.values():
                self._dep(o, r)
            for r in b.readers.values():
                self._dep(o, r)
        key = dsem if dsem is not None else eng
        for b in reads:
            b.readers[key] = o
        for b in writes:
            b.last_w = o
            b.writers[key] = o
        self.ops[eng].append(o)
        return o

    def fence(self, eng, bufs):
        self.op(eng, None, reads=(), writes=bufs)

    def emit(self):
        nc = self.nc
        st = self.stack
        esem = {e: st.enter_context(nc.semaphore("sem_" + e)) for e in ENGS}
        for d in self.dsems:
            d.h = st.enter_context(nc.semaphore("dsem_" + d.name))
        for e in ENGS:
            c = 0
            for o in self.ops[e]:
                if o.dsem is None and o.signal:
                    c += 1
                o.val = c
        ops = self.ops

        def run(e, eng):
            for o in ops[e]:
                for y in o.waits:
                    if y.dsem is not None:
                        eng.wait_ge(y.dsem.h, y.dval)
                    else:
                        eng.wait_ge(esem[y.eng], y.val)
                if o.fn is None:
                    continue
                ins = o.fn(eng)
                if o.dsem is not None:
                    ins.then_inc(o.dsem.h, 16)
                elif o.signal:
                    ins.then_inc(esem[e], 1)

        block = st.enter_context(nc.Block())

        @block.tensor
        def _(eng):
            run("pe", eng)

        @block.scalar
        def _(eng):
            run("act", eng)

        @block.vector
        def _(eng):
            run("dve", eng)

        @block.gpsimd
        def _(eng):
            run("pool", eng)

        @block.sync
        def _(eng):
            run("sp", eng)

    def close(self):
        self.stack.close()


from concourse.bass_utils import run_bass_kernel_spmd
import os
SKIP = os.environ.get('K_SKIP', '').split(',')

T = 512
NT = 8
ALPHA = 2.0 ** 0.5
LN_EPS = 1e-5
NSLAB = 3
SLAB = 5632
NF = 10
BIGV = 1e30
NV = 176
C_LNG, C_LNB, C_CW, C_CB, C_GB, C_LAM, C_NSP8, C_NSP16, C_TMP = 0, 48, 96, 136, 146, 166, 176, 186, 196
GELU_C = 1.5957691216057308


def build_tables():
    jk = np.arange(128)[:, None]
    jq = np.arange(128)[None, :]
    tabs = np.full((128, 21, 128), BIGV, np.float32)
    order = [(4, 0), (16, 0), (4, 1), (16, 1), (16, 2), (16, 3), (16, 4)]
    for idx, (d, dl) in enumerate(order):
        m = jq - jk + 128 * dl
        if d == 4:
            valid = (m >= 0) & (m <= 128)
        else:
            valid = (m >= 0) & (m <= 512) & (m % 4 == 0)
        tabs[:, idx, :] = np.where(valid, 4 * m, BIGV)
    for dl in (0, 1):
        for de in range(-3, 4):
            dist = 4 * (jq - jk) + de + 512 * dl
            tabs[:, 7 + 7 * dl + de + 3, :] = np.where((dist >= 0) & (dist <= 128), dist, BIGV)
    return tabs.reshape(128, 21 * 128)


class Rot:
    def __init__(self, items):
        self.items = list(items)
        self.i = 0

    def next(self):
        it = self.items[self.i % len(self.items)]
        self.i += 1
        return it


def build_program(ntiles=NT, stop=None):
    nc = bass.Bass("TRN2", target_bir_lowering=False)
    P = Prog(nc)

    def din(name, shape):
        return nc.dram_tensor(name, list(shape), F32, kind="ExternalInput").ap()

    x_d = din("x", [4096, 1024])
    y_d = nc.dram_tensor("y", [4096, 1024], F32, kind="ExternalOutput").ap()
    w_ffn_in = din("ffn_w_in", [2, 2, 1024, 5632])
    w_ffn_out = din("ffn_w_out", [2, 2, 2816, 1024])
    w_rg_in = din("rg_w_in", [1024, 2560])
    w_gate = din("rg_gate_w", [2, 5, 256, 256])
    w_rg_out = din("rg_w_out", [1280, 1024])
    w_kv = din("kv_w", [1024, 2048])
    w_q = din("attn_w_q", [1024, 3072])
    w_o = din("attn_w_o", [1024, 1024])
    vecs_d = din("vecs", [128, NV])
    tabs_d = din("tabs", [128, 21 * 128])
    ident_d = din("ident", [128, 128])

    slabs = [P.sbuf(f"slab{i}", [128, SLAB], BF16) for i in range(NSLAB)]
    slabB = [Buf(f"slab{i}") for i in range(NSLAB)]
    slabD = [P.dsem(f"slab{i}") for i in range(NSLAB)]
    KT = P.sbuf("KT", [128, 8, 5 * T], BF16)
    KTB = [[Buf(f"KT{c}_{s}") for s in range(5)] for c in range(8)]
    V = P.sbuf("V", [128, 320, 65], BF16)
    VB = [[[Buf(f"V{s}_{r}_{hv}") for hv in range(2)] for r in range(4)] for s in range(5)]
    tabs = P.sbuf("tabs", [128, 21, 128], F32)
    tabsB = Buf("tabs")
    vecs = P.sbuf("vecs", [128, 208], F32)
    vecsB = Buf("vecs")
    ident = P.sbuf("ident", [128, 128], F32)
    identB = Buf("ident")
    identb = P.sbuf("identb", [128, 128], BF16)
    onesb = P.sbuf("onesb", [128, 128], BF16)
    cst = P.sbuf("cst", [128, 4], F32)
    cstB = Buf("cst")
    xs = P.sbuf("xs", [128, 8, T], F32)
    xsB = [Buf(f"xs{c}") for c in range(8)]
    xb = P.sbuf("xb", [128, 8, T], BF16)
    xbB = [Buf(f"xb{c}") for c in range(8)]
    big = P.sbuf("big", [128, 22, T], BF16)
    bigB = [Buf(f"big{c}") for c in range(22)]
    fp = P.sbuf("fp", [128, NF, T], F32)
    fpB = [Buf(f"fp{c}") for c in range(NF)]
    ubr = [P.sbuf(f"ubr{i}", [128, 3 + T], F32) for i in range(2)]
    ubrB = [Buf(f"ubr{i}") for i in range(2)]
    Eb = [P.sbuf(f"E{i}", [128, 512], BF16) for i in range(4)]
    EB = [Buf(f"E{i}") for i in range(4)]
    hprev = P.sbuf("hprev", [128, 10], F32)
    hprevB = [Buf(f"hprev{c}") for c in range(10)]
    halo = P.sbuf("halo", [128, 10, 3], F32)
    haloB = [Buf(f"halo{c}") for c in range(10)]
    rden = P.sbuf("rden", [128, 8], F32)
    rdenB = [Buf(f"rden{c}") for c in range(8)]
    pb = [P.psum(f"pb{i}", [128, 512], F32) for i in range(4)]
    pbB = [Buf(f"pb{i}") for i in range(4)]
    po = [P.psum(f"po{i}", [128, 512], F32) for i in range(4)]
    poB = [[Buf(f"po{i}")] for i in range(4)]
    psR5 = Rot(list(zip(pb, pbB)))
    psR = Rot(list(zip(pb, pbB)) + [(po[k], poB[k][0]) for k in range(2)])

    def fp_rot(idx):
        return Rot([(fp[:, k, :], fpB[k]) for k in idx])

    d_tabs, d_vecs, d_ident = P.dsem("tabs"), P.dsem("vecs"), P.dsem("ident")
    P.op("sp", lambda e: e.dma_start(out=tabs[:], in_=tabs_d.rearrange("p (a b) -> p a b", a=21)), writes=[tabsB], dsem=d_tabs)
    P.op("sp", lambda e: e.dma_start(out=vecs[:, 0:NV], in_=vecs_d), writes=[vecsB], dsem=d_vecs)
    P.op("sp", lambda e: e.dma_start(out=ident[:], in_=ident_d), writes=[identB], dsem=d_ident)
    P.op("dve", lambda e: e.memset(onesb[:], 1.0 / 1024.0), writes=[identB])
    P.op("dve", lambda e: e.tensor_copy(out=identb[:], in_=ident[:]), reads=[identB], writes=[identB])
    P.op("dve", lambda e: e.memset(cst[:, 0:1], LN_EPS), writes=[cstB])
    P.op("dve", lambda e: e.memset(cst[:, 1:2], 4.0 * LN_EPS), writes=[cstB])
    P.op("dve", lambda e: e.memset(cst[:, 2:3], 1.0), writes=[cstB])
    P.op("dve", lambda e: e.memset(hprev[:], 0.0), writes=hprevB)
    P.op("dve", lambda e: e.memset(halo[:], 0.0), writes=haloB)
    allVB = [b for s in VB for r in s for b in r]
    if 'vmem' not in SKIP:
        P.op("dve", lambda e: e.memset(V[:, :, 64:65], 1.0), writes=allVB)
    if 'lam' not in SKIP:
        P.op("act", lambda e: e.activation(out=vecs[:, C_TMP:C_TMP + 10], in_=vecs[:, C_LAM:C_LAM + 10], func=AF.Exp, scale=-1.0), reads=[vecsB], writes=[vecsB])
        P.op("act", lambda e: e.activation(out=vecs[:, C_TMP:C_TMP + 10], in_=vecs[:, C_TMP:C_TMP + 10], func=AF.Ln, bias=cst[:, 2:3], scale=1.0), reads=[vecsB, cstB], writes=[vecsB])
    P.op("dve", lambda e: e.tensor_scalar(out=vecs[:, C_NSP8:C_NSP8 + 10], in0=vecs[:, C_TMP:C_TMP + 10], scalar1=-8.0, scalar2=None, op0=ALU.mult), reads=[vecsB], writes=[vecsB])
    P.op("dve", lambda e: e.tensor_scalar(out=vecs[:, C_NSP16:C_NSP16 + 10], in0=vecs[:, C_TMP:C_TMP + 10], scalar1=-16.0, scalar2=None, op0=ALU.mult), reads=[vecsB], writes=[vecsB])

    slab_ctr = [0]

    def load_slab(parts):
        k = slab_ctr[0] % NSLAB
        slab_ctr[0] += 1
        t = slabs[k]
        first = True
        for ov, ia in parts:
            o_ap = ov(t)
            if first:
                P.op("pool", lambda e, o=o_ap, i=ia: e.dma_start(out=o, in_=i), writes=[slabB[k]], dsem=slabD[k])
                first = False
            else:
                o = P.op("pool", lambda e, o=o_ap, i=ia: e.dma_start(out=o, in_=i), dsem=slabD[k])
                slabB[k].last_w = o
                slabB[k].writers[slabD[k]] = o
        return t, slabB[k]

    def mm(out_ap, lhsT, rhs, start, stop, reads, writes):
        P.op("pe", lambda e: e.matmul(out_ap, lhsT=lhsT, rhs=rhs, start=start, stop=stop), reads=reads, writes=writes)

    def sq_view(m):
        return fp[:, m // 2, :].bitcast(BF16)[:, (m % 2) * T:(m % 2 + 1) * T]

    def ln_chunk(m):
        P.op("act", lambda e: e.activation(out=sq_view(m), in_=xs[:, m, :], func=AF.Square), reads=[xsB[m]], writes=[fpB[m // 2]])
        P.op("dve", lambda e: e.tensor_copy(out=xb[:, m, :], in_=xs[:, m, :]), reads=[xsB[m]], writes=[xbB[m]])

    def ln_stats(m):
        mm(po[2][:], onesb[:], xb[:, m, :], m == 0, m == 7, [identB, xbB[m]], [poB[2][0]])
        mm(po[3][:], onesb[:], sq_view(m), m == 0, m == 7, [identB, fpB[m // 2]], [poB[3][0]])

    def ln_finish(ln_idx, eps_col):
        pm, pmB, pq, pqB = po[2], poB[2][0], po[3], poB[3][0]
        mean, meanB = fp[:, 4, :], fpB[4]
        var, varB = fp[:, 5, :], fpB[5]
        P.op("act", lambda e: e.copy(out=mean, in_=pm[:]), reads=[pmB], writes=[meanB])
        P.op("dve", lambda e: e.tensor_tensor(out=var, in0=mean, in1=mean, op=ALU.mult), reads=[meanB], writes=[varB])
        P.op("dve", lambda e: e.tensor_tensor(out=var, in0=pq[:], in1=var, op=ALU.subtract), reads=[pqB, varB], writes=[varB])
        P.op("act", lambda e: e.activation(out=var, in_=var, func=AF.Sqrt, bias=cst[:, eps_col:eps_col + 1], scale=1.0), reads=[varB, cstB], writes=[varB])
        P.op("dve", lambda e: e.reciprocal(out=var, in_=var), reads=[varB], writes=[varB])
        for c in range(8):
            g = vecs[:, C_LNG + ln_idx * 8 + c:C_LNG + ln_idx * 8 + c + 1]
            bb = vecs[:, C_LNB + ln_idx * 8 + c:C_LNB + ln_idx * 8 + c + 1]
            P.op("dve", lambda e, c=c: e.tensor_tensor(out=xs[:, c, :], in0=xs[:, c, :], in1=mean, op=ALU.subtract), reads=[meanB, xsB[c]], writes=[xsB[c]])
            P.op("dve", lambda e, c=c: e.tensor_tensor(out=xs[:, c, :], in0=xs[:, c, :], in1=var, op=ALU.mult), reads=[varB, xsB[c]], writes=[xsB[c]])
            P.op("act", lambda e, c=c, g=g, bb=bb: e.activation(out=xb[:, c, :], in_=xs[:, c, :], func=AF.Identity, bias=bb, scale=g), reads=[xsB[c], vecsB], writes=[xbB[c]])
            P.op("act", lambda e, c=c, g=g, bb=bb: e.activation(out=xs[:, c, :], in_=xs[:, c, :], func=AF.Identity, bias=bb, scale=g), reads=[xsB[c], vecsB], writes=[xsB[c]])

    def residual_ln(chunk_mm, scalar, ln_idx, eps_col):
        for m in range(8):
            ps, psB = chunk_mm(m)
            P.op("dve", lambda e, m=m, ps=ps: e.scalar_tensor_tensor(out=xs[:, m, :], in0=xs[:, m, :], scalar=scalar, in1=ps[:], op0=ALU.mult, op1=ALU.add), reads=[xsB[m], psB], writes=[xsB[m]])
            ln_chunk(m)
            if m > 0:
                ln_stats(m - 1)
        ln_stats(7)
        ln_finish(ln_idx, eps_col)

    def mm_wave(accs, nk, rhs_fn, rhsB_fn):
        for kc in range(nk):
            for ps_ap, psB, lf, rd in accs:
                mm(ps_ap, lf(kc), rhs_fn(kc), kc == 0, kc == nk - 1, rd + [rhsB_fn(kc)], [psB])

    def ffn(l, i2):
        win = w_ffn_in[l, i2].rearrange("(kc p) (h f) -> p kc h f", p=128, h=2)
        wout = w_ffn_out[l, i2].rearrange("(kc p) n -> p kc n", p=128)
        fr = fp_rot([0, 1, 2])
        for s in range(11):
            def ov(t, h):
                return t[:, 0:4096].rearrange("p (kc h f) -> p kc h f", kc=8, h=2)[:, :, h, :]
            sl, slB = load_slab([(lambda t: ov(t, 0), win[:, :, 0, s * 256:(s + 1) * 256]),
                                 (lambda t: ov(t, 1), win[:, :, 1, s * 256:(s + 1) * 256])])
            sv = sl[:, 0:4096].rearrange("p (kc h f) -> p kc h f", kc=8, h=2)
            accs = []
            for jj in range(2):
                pg, pgB = psR.next()
                pu, puB = psR.next()
                accs.append((pg[:], pgB, (lambda kc, jj=jj, sv=sv: sv[:, kc, 0, jj * 128:(jj + 1) * 128]), [slB]))
                accs.append((pu[:], puB, (lambda kc, jj=jj, sv=sv: sv[:, kc, 1, jj * 128:(jj + 1) * 128]), [slB]))
            if s == 0:
                mm_wave(accs, 8, lambda kc: xb[:, kc, :], lambda kc: xbB[kc])
            else:
                mm_wave(accs[0:2], 8, lambda kc: xb[:, kc, :], lambda kc: xbB[kc])
                mm_wave(accs[2:4], 8, lambda kc: xb[:, kc, :], lambda kc: xbB[kc])
            for jj in range(2):
                j = 2 * s + jj
                pg_ap, pgB = accs[2 * jj][0], accs[2 * jj][1]
                pu_ap, puB = accs[2 * jj + 1][0], accs[2 * jj + 1][1]
                sg, sgB = fr.next()
                P.op("act", lambda e, sg=sg, pg_ap=pg_ap: e.activation(out=sg, in_=pg_ap, func=AF.Silu), reads=[pgB], writes=[sgB])
                P.op("dve", lambda e, sg=sg, pu_ap=pu_ap, j=j: e.tensor_tensor(out=big[:, j, :], in0=sg, in1=pu_ap, op=ALU.mult), reads=[sgB, puB], writes=[bigB[j]])
        slabs_out = {}

        def chunk_mm(m):
            s, mmi = m // 2, m % 2
            if mmi == 0:
                def ov2(t, a, bb):
                    return t[:, 0:5632].rearrange("p (kc f) -> p kc f", kc=22)[:, a:bb, :]
                slabs_out[s] = load_slab([(lambda t: ov2(t, 0, 11), wout[:, 0:11, s * 256:(s + 1) * 256]),
                                          (lambda t: ov2(t, 11, 22), wout[:, 11:22, s * 256:(s + 1) * 256])])
            sl, slB = slabs_out[s]
            sv = sl[:, 0:5632].rearrange("p (kc f) -> p kc f", kc=22)
            ps, psB = psR.next()
            for kc in range(22):
                mm(ps[:], sv[:, kc, mmi * 128:(mmi + 1) * 128], big[:, kc, :], kc == 0, kc == 21, [slB, bigB[kc]], [psB])
            return ps, psB

        residual_ln(chunk_mm, 2.0 * ALPHA, l * 3 + (0 if i2 == 0 else 2), 1)

    def rglru():
        win = w_rg_in.rearrange("(kc p) (h f) -> p kc h f", p=128, h=2)
        fr = fp_rot(list(range(NF)))
        ubR = Rot(list(zip(ubr, ubrB)))
        yh = big
        for n in range(5):
            def ov(t, h):
                return t[:, 0:4096].rearrange("p (kc h f) -> p kc h f", kc=8, h=2)[:, :, h, :]

            def ovg(t, g):
                return t[:, 4096 + g * 512:4096 + (g + 1) * 512].rearrange("p (a d) -> p a d", a=2)
            sl, slB = load_slab([(lambda t: ov(t, 0), win[:, :, 0, n * 256:(n + 1) * 256]),
                                 (lambda t: ov(t, 1), win[:, :, 1, n * 256:(n + 1) * 256]),
                                 (lambda t: ovg(t, 0), w_gate[0, n].rearrange("(kc p) d -> p kc d", p=128)),
                                 (lambda t: ovg(t, 1), w_gate[1, n].rearrange("(kc p) d -> p kc d", p=128))])
            sv = sl[:, 0:4096].rearrange("p (kc h f) -> p kc h f", kc=8, h=2)
            gv = sl[:, 4096:5120].rearrange("p (a d) -> p a d", a=4)
            gslB = slB
            pys, pus, accs = [], [], []
            for q in range(2):
                py, pyB = psR.next()
                pu, puB = psR.next()
                pys.append((py, pyB))
                pus.append((pu, puB))
                accs.append((py[:], pyB, (lambda kc, q=q, sv=sv: sv[:, kc, 0, q * 128:(q + 1) * 128]), [slB]))
                accs.append((pu[:], puB, (lambda kc, q=q, sv=sv: sv[:, kc, 1, q * 128:(q + 1) * 128]), [slB]))
            mm_wave(accs, 8, lambda kc: xb[:, kc, :], lambda kc: xbB[kc])
            ys = [fr.next() for q in range(2)]
            us = [fr.next() for q in range(2)]
            ubs = [ubR.next() for q in range(2)]
            Q2 = (0, 1)
            chs = [2 * n, 2 * n + 1]
            for q in Q2:
                P.op("act", lambda e, y=ys[q][0], py=pys[q][0]: e.activation(out=y, in_=py[:], func=AF.Square), reads=[pys[q][1]], writes=[ys[q][1]])
            for q in Q2:
                P.op("act", lambda e, ub_=ubs[q][0], pu=pus[q][0]: e.copy(out=ub_[:, 3:3 + T], in_=pu[:]), reads=[pus[q][1]], writes=[ubs[q][1]])
                P.op("act", lambda e, ub_=ubs[q][0], ch=chs[q]: e.copy(out=ub_[:, 0:3], in_=halo[:, ch, :]), reads=[haloB[chs[q]]], writes=[ubs[q][1]])
            for q in Q2:
                P.op("dve", lambda e, y=ys[q][0]: e.tensor_scalar(out=y, in0=y, scalar1=0.044715, scalar2=1.0, op0=ALU.mult, op1=ALU.add), reads=[ys[q][1]], writes=[ys[q][1]])
            for q in Q2:
                P.op("dve", lambda e, y=ys[q][0], py=pys[q][0]: e.tensor_tensor(out=y, in0=y, in1=py[:], op=ALU.mult), reads=[ys[q][1], pys[q][1]], writes=[ys[q][1]])
            for q in Q2:
                P.op("act", lambda e, y=ys[q][0]: e.activation(out=y, in_=y, func=AF.Sigmoid, scale=GELU_C), reads=[ys[q][1]], writes=[ys[q][1]])
            cw = lambda w_, ch: vecs[:, C_CW + w_ * 10 + ch:C_CW + w_ * 10 + ch + 1]
            for q in Q2:
                P.op("dve", lambda e, u=us[q][0], ub_=ubs[q][0], ch=chs[q]: e.tensor_scalar(out=u, in0=ub_[:, 3:3 + T], scalar1=cw(3, ch), scalar2=vecs[:, C_CB + ch:C_CB + ch + 1], op0=ALU.mult, op1=ALU.add), reads=[ubs[q][1], vecsB], writes=[us[q][1]])
            for w_ in range(3):
                for q in Q2:
                    P.op("dve", lambda e, u=us[q][0], ub_=ubs[q][0], ch=chs[q], w_=w_: e.scalar_tensor_tensor(out=u, in0=ub_[:, w_:w_ + T], scalar=cw(w_, ch), in1=u, op0=ALU.mult, op1=ALU.add), reads=[ubs[q][1], us[q][1], vecsB], writes=[us[q][1]])
            for q in Q2:
                P.op("act", lambda e, u=us[q][0], q=q: e.copy(out=big[:, 20 + q, :], in_=u), reads=[us[q][1]], writes=[bigB[20 + q]])
                P.op("act", lambda e, ub_=ubs[q][0], ch=chs[q]: e.copy(out=halo[:, ch, :], in_=ub_[:, T:T + 3]), reads=[ubs[q][1]], writes=[haloB[chs[q]]])
            for q in Q2:
                P.op("dve", lambda e, y=ys[q][0], py=pys[q][0]: e.tensor_tensor(out=y, in0=y, in1=py[:], op=ALU.mult), reads=[ys[q][1], pys[q][1]], writes=[ys[q][1]])
            pgs = {}
            for g in range(2):
                for q in Q2:
                    ps, psB = psR.next()
                    for k2 in range(2):
                        mm(ps[:], gv[:, g * 2 + k2, q * 128:(q + 1) * 128], big[:, 20 + k2, :], k2 == 0, k2 == 1, [gslB, bigB[20 + k2]], [psB])
                    pgs[(g, q)] = (ps, psB)
            As = [fr.next() for q in Q2]
            Bs = [fr.next() for q in Q2]
            Cs = [fr.next() for q in Q2]
            gb = lambda g, ch: vecs[:, C_GB + g * 10 + ch:C_GB + g * 10 + ch + 1]
            for q in Q2:
                P.op("act", lambda e, a=As[q][0], ps=pgs[(0, q)][0], ch=chs[q]: e.activation(out=a, in_=ps[:], func=AF.Sigmoid, bias=gb(0, ch), scale=1.0), reads=[pgs[(0, q)][1], vecsB], writes=[As[q][1]])
            for q in Q2:
                P.op("act", lambda e, c3=Cs[q][0], ps=pgs[(1, q)][0], ch=chs[q]: e.activation(out=c3, in_=ps[:], func=AF.Sigmoid, bias=gb(1, ch), scale=1.0), reads=[pgs[(1, q)][1], vecsB], writes=[Cs[q][1]])
            for q in Q2:
                P.op("act", lambda e, a=As[q][0], b2=Bs[q][0], ch=chs[q]: e.activation(out=b2, in_=a, func=AF.Exp, scale=vecs[:, C_NSP16 + ch:C_NSP16 + ch + 1]), reads=[As[q][1], vecsB], writes=[Bs[q][1]])
            for q in Q2:
                P.op("act", lambda e, a=As[q][0], ch=chs[q]: e.activation(out=a, in_=a, func=AF.Exp, scale=vecs[:, C_NSP8 + ch:C_NSP8 + ch + 1]), reads=[As[q][1], vecsB], writes=[As[q][1]])
            for q in Q2:
                P.op("dve", lambda e, c3=Cs[q][0], u=us[q][0]: e.tensor_tensor(out=c3, in0=c3, in1=u, op=ALU.mult), reads=[Cs[q][1], us[q][1]], writes=[Cs[q][1]])
            for q in Q2:
                P.op("act", lambda e, b2=Bs[q][0]: e.activation(out=b2, in_=b2, func=AF.Sqrt, bias=cst[:, 2:3], scale=-1.0), reads=[Bs[q][1], cstB], writes=[Bs[q][1]])
            for q in Q2:
                P.op("dve", lambda e, b2=Bs[q][0], c3=Cs[q][0]: e.tensor_tensor(out=b2, in0=b2, in1=c3, op=ALU.mult), reads=[Bs[q][1], Cs[q][1]], writes=[Bs[q][1]])
            for q in Q2:
                P.op("dve", lambda e, a=As[q][0], b2=Bs[q][0], c3=Cs[q][0], ch=chs[q]: e.tensor_tensor_scan(out=c3, data0=a, data1=b2, initial=hprev[:, ch:ch + 1], op0=ALU.mult, op1=ALU.add), reads=[As[q][1], Bs[q][1], hprevB[chs[q]]], writes=[Cs[q][1]])
            for q in Q2:
                P.op("act", lambda e, c3=Cs[q][0], ch=chs[q]: e.copy(out=hprev[:, ch:ch + 1], in_=c3[:, T - 1:T]), reads=[Cs[q][1]], writes=[hprevB[chs[q]]])
            for q in Q2:
                P.op("dve", lambda e, y=ys[q][0], c3=Cs[q][0], ch=chs[q]: e.tensor_tensor(out=yh[:, ch, :], in0=y, in1=c3, op=ALU.mult), reads=[ys[q][1], Cs[q][1]], writes=[bigB[chs[q]]])
        wout = w_rg_out.rearrange("(kc p) n -> p kc n", p=128)
        slabs_out = {}

        def chunk_mm(m):
            s, mmi = m // 4, m % 4
            if mmi == 0:
                slabs_out[s] = load_slab([(lambda t: t[:, 0:5120].rearrange("p (kc f) -> p kc f", kc=10), wout[:, :, s * 512:(s + 1) * 512])])
            sl, slB = slabs_out[s]
            sv = sl[:, 0:5120].rearrange("p (kc f) -> p kc f", kc=10)
            ps, psB = psR.next()
            for kc in range(10):
                mm(ps[:], sv[:, kc, mmi * 128:(mmi + 1) * 128], yh[:, kc, :], kc == 0, kc == 9, [slB, bigB[kc]], [psB])
            return ps, psB

        residual_ln(chunk_mm, ALPHA, 1, 0)

    def kvproj(i):
        slot = i % 5
        wk = w_kv.rearrange("(kc p) n -> p kc n", p=128)
        for s in range(2):
            sl, slB = load_slab([(lambda t: t[:, 0:4096].rearrange("p (kc f) -> p kc f", kc=8), wk[:, :, s * 512:(s + 1) * 512])])
            sv = sl[:, 0:4096].rearrange("p (kc f) -> p kc f", kc=8)
            accs = []
            for mmi in range(4):
                ps, psB = psR.next()
                accs.append((ps[:], psB, (lambda kc, mmi=mmi, sv=sv: sv[:, kc, mmi * 128:(mmi + 1) * 128]), [slB]))
            mm_wave(accs, 8, lambda kc: xb[:, kc, :], lambda kc: xbB[kc])
            for mmi in range(4):
                c = 4 * s + mmi
                ps_ap, psB = accs[mmi][0], accs[mmi][1]
                if mmi % 2 == 0:
                    P.op("act", lambda e, c=c, ps_ap=ps_ap: e.copy(out=KT[:, c, slot * T:(slot + 1) * T].rearrange("p (r j) -> p r j", r=4), in_=ps_ap.rearrange("p (j r) -> p r j", r=4)), reads=[psB], writes=[KTB[c][slot]])
                else:
                    P.op("dve", lambda e, c=c, ps_ap=ps_ap: e.tensor_copy(out=KT[:, c, slot * T:(slot + 1) * T].rearrange("p (r j) -> p r j", r=4), in_=ps_ap.rearrange("p (j r) -> p r j", r=4)), reads=[psB], writes=[KTB[c][slot]])
        for hv in range(2):
            sl, slB = load_slab([(lambda t: t[:, 0:4096].rearrange("p (kc f) -> p kc f", kc=8), wk[:, :, 1024 + hv * 512:1024 + (hv + 1) * 512])])
            sv = sl[:, 0:4096].rearrange("p (kc f) -> p kc f", kc=8)
            for r4 in range(4):
                ps, psB = psR.next()
                for kc in range(8):
                    mm(ps[:], xb[:, kc, r4:T:4], sv[:, kc, :], kc == 0, kc == 7, [slB, xbB[kc]], [psB])
                base = slot * 64 + r4 * 16 + hv * 8
                P.op("dve", lambda e, ps=ps, base=base: e.tensor_copy(out=V[:, base:base + 8, 0:64], in_=ps[:].rearrange("p (h d) -> p h d", h=8)), reads=[psB], writes=[VB[slot][r4][hv]])

    def attention(i):
        wq = w_q.rearrange("(kc p) (g f) -> p kc g f", p=128, g=3)
        QT = [big[:, 16:19, :], big[:, 19:22, :]]
        QTB = [bigB[16:19], bigB[19:22]]
        AO = big[:, 8:16, :].rearrange("p a t -> p (a t)").rearrange("p (r f) -> p r f", r=4)
        AT = big[:, 0:8, :]
        smR = fp_rot([0, 1, 2, 3, 4, 5])
        ER = Rot(list(zip(Eb, EB)))
        gen = [(1, 0), (2, 0), (1, 1), (2, 1), (2, 2), (2, 3), (2, 4)]
        nvalid = [2, 4, 5, 6, 7][min(i, 4)]
        def qproj(c):
            def ov(t, g):
                return t[:, 0:3072].rearrange("p (kc g f) -> p kc g f", kc=8, g=3)[:, :, g, :]
            sl, slB = load_slab([(lambda t, g=g: ov(t, g), wq[:, :, g, c * 128:(c + 1) * 128]) for g in range(3)])
            sv = sl[:, 0:3072].rearrange("p (kc g f) -> p kc g f", kc=8, g=3)
            qt, qtB = QT[c % 2], QTB[c % 2]
            accs = []
            for g in range(3):
                ps, psB = psR5.next()
                accs.append((ps[:], psB, (lambda kc, g=g, sv=sv: sv[:, kc, g, :]), [slB]))
            mm_wave(accs, 8, lambda kc: xb[:, kc, :], lambda kc: xbB[kc])
            for g in range(3):
                ps, psB = accs[g][0], accs[g][1]
                if g == 1:
                    P.op("dve", lambda e, ps=ps, qt=qt, g=g: e.tensor_copy(out=qt[:, g, :].rearrange("p (r j) -> p r j", r=4), in_=ps.rearrange("p (j r) -> p r j", r=4)), reads=[psB], writes=[qtB[g]])
                else:
                    P.op("act", lambda e, ps=ps, qt=qt, g=g: e.copy(out=qt[:, g, :].rearrange("p (r j) -> p r j", r=4), in_=ps.rearrange("p (j r) -> p r j", r=4)), reads=[psB], writes=[qtB[g]])

        qproj(0)
        for c in range(8):
            qt, qtB = QT[c % 2], QTB[c % 2]
            jobs = []
            for hh in range(2):
                d1 = [(dl, rk) for dl in (1, 0) if i - dl >= 0 for rk in (3, 2, 1, 0)]
                for k_, (dl, rk) in enumerate(d1):
                    jobs.append(("d1", hh, dl, rk, k_ == 0))
                for r4 in range(4):
                    nb_a = 2 if i == 0 else 4
                    jobs.append(("genA", hh, r4, nb_a, nvalid <= 4))
                    if nvalid > 4:
                        jobs.append(("genB", hh, r4, nvalid - 4, True))
            state = {}

            def emit_qk(jn):
                job = jobs[jn]
                kind, hh = job[0], job[1]
                h = 2 * c + hh
                pbase = 64 * hh
                slope8 = -8.0 * (2.0 ** (-(h + 1) / 2.0))
                ps, psB = psR5.next()
                if kind == "d1":
                    dl, rk = job[2], job[3]
                    slot = (i - dl) % 5
                    mm(ps[:], KT[pbase:pbase + 64, c, slot * T + rk * 128:slot * T + (rk + 1) * 128], qt[pbase:pbase + 64, 0, :], True, True, [KTB[c][slot], qtB[0]], [psB])
                    ncol = 512
                    t0 = 10 + 7 * dl - rk
                    nb = 4
                elif kind == "genA":
                    r4, nb = job[2], job[3]
                    for dl in range(nb // 2):
                        slot = (i - dl) % 5
                        mm(ps[:, dl * 256:(dl + 1) * 256], KT[pbase:pbase + 64, c, slot * T + r4 * 128:slot * T + (r4 + 1) * 128], qt[pbase:pbase + 64, 1:3, r4 * 128:(r4 + 1) * 128], True, True, [KTB[c][slot], qtB[1], qtB[2]], [psB])
                    ncol = nb * 128
                    t0 = 0
                else:
                    r4, nb = job[2], job[3]
                    for bi in range(nb):
                        slot = (i - (2 + bi)) % 5
                        mm(ps[:, bi * 128:(bi + 1) * 128], KT[pbase:pbase + 64, c, slot * T + r4 * 128:slot * T + (r4 + 1) * 128], qt[pbase:pbase + 64, 2, r4 * 128:(r4 + 1) * 128], True, True, [KTB[c][slot], qtB[2]], [psB])
                    ncol = nb * 128
                    t0 = 4
                sm, smB = smR.next()
                P.op("dve", lambda e, sm=sm, ps=ps, nb=nb, t0=t0, ncol=ncol, slope8=slope8: e.scalar_tensor_tensor(out=sm[:, 0:ncol], in0=tabs[:, t0:t0 + nb, :].rearrange("p a b -> p (a b)"), scalar=slope8, in1=ps[:, 0:ncol], op0=ALU.mult, op1=ALU.add), reads=[tabsB, psB], writes=[smB])
                E, EBuf = ER.next()
                P.op("act", lambda e, E=E, sm=sm, ncol=ncol: e.activation(out=E[:, 0:ncol], in_=sm[:, 0:ncol], func=AF.Exp, scale=0.125), reads=[smB], writes=[EBuf])
                state[jn] = (E, EBuf)

            def emit_pv(jn):
                job = jobs[jn]
                kind, hh = job[0], job[1]
                h = 2 * c + hh
                E, EBuf = state.pop(jn)
                if kind == "d1":
                    dl, rk, first = job[2], job[3], job[4]
                    slot = (i - dl) % 5
                    vix = slot * 64 + rk * 16 + h
                    for r4 in range(4):
                        mm(po[r4][:, 0:65], E[:, r4 * 128:(r4 + 1) * 128], V[:, vix, :], first, False, [EBuf, VB[slot][rk][h // 8]], [poB[r4][0]])
                    return
                r4, nb, last = job[2], job[3], job[4]
                acc = po[r4][:, 0:65]
                accB = poB[r4][0]
                for bi in range(nb):
                    if kind == "genA":
                        dl = bi // 2
                    else:
                        dl = 2 + bi
                    slot = (i - dl) % 5
                    vix = slot * 64 + r4 * 16 + h
                    mm(acc, E[:, bi * 128:(bi + 1) * 128], V[:, vix, :], False, last and bi == nb - 1, [EBuf, VB[slot][r4][h // 8]], [accB])
                if last:
                    rd = rden[:, hh * 4 + r4:hh * 4 + r4 + 1]
                    rdB = rdenB[hh * 4 + r4]
                    P.op("dve", lambda e, rd=rd, acc=acc: e.reciprocal(out=rd, in_=acc[:, 64:65]), reads=[accB], writes=[rdB])
                    P.op("act", lambda e, rd=rd, acc=acc, r4=r4, h=h: e.activation(out=AO[:, r4, h * 64:(h + 1) * 64], in_=acc[:, 0:64], func=AF.Identity, scale=rd), reads=[accB, rdB], writes=[bigB[8 + 2 * r4], bigB[9 + 2 * r4]])

            LOOK = 3
            for jn in range(len(jobs) + LOOK):
                if jn == len(jobs) - 6 and c + 1 < 8:
                    qproj(c + 1)
                if jn < len(jobs):
                    emit_qk(jn)
                if jn - LOOK >= 0:
                    emit_pv(jn - LOOK)
            ptf, pTB = psR5.next()
            pT = ptf[:].bitcast(BF16)
            for r4 in range(4):
                P.op("pe", lambda e, r4=r4, c=c, pT=pT: e.transpose(out=pT[:, r4 * 128:(r4 + 1) * 128], in_=AO[:, r4, c * 128:(c + 1) * 128], identity=identb[:]), reads=[bigB[8 + 2 * r4], bigB[9 + 2 * r4], identB], writes=[pTB])
            P.op("dve", lambda e, c=c, pT=pT: e.tensor_copy(out=AT[:, c, :].rearrange("p (j r) -> p r j", r=4), in_=pT[:, 0:512].rearrange("p (r j) -> p r j", r=4)), reads=[pTB], writes=[bigB[c]])
        wo = w_o.rearrange("(kc p) n -> p kc n", p=128)
        slabs_out = {}

        def chunk_mm(m):
            s, mmi = m // 4, m % 4
            if mmi == 0:
                slabs_out[s] = load_slab([(lambda t: t[:, 0:4096].rearrange("p (kc f) -> p kc f", kc=8), wo[:, :, s * 512:(s + 1) * 512])])
            sl, slB = slabs_out[s]
            sv = sl[:, 0:4096].rearrange("p (kc f) -> p kc f", kc=8)
            ps, psB = psR.next()
            for kc in range(8):
                mm(ps[:], sv[:, kc, mmi * 128:(mmi + 1) * 128], AT[:, kc, :], kc == 0, kc == 7, [slB, bigB[kc]], [psB])
            return ps, psB

        residual_ln(chunk_mm, ALPHA, 4, 0)

    xin = [fp[:, 6:8, :].rearrange("p a t -> p (a t)"), fp[:, 8:10, :].rearrange("p a t -> p (a t)")]
    xinB = [fpB[6:8], fpB[8:10]]
    xinD = [P.dsem("xin0"), P.dsem("xin1")]
    xo = [fp[:, 0:2, :].rearrange("p a t -> p (a t)"), fp[:, 2:4, :].rearrange("p a t -> p (a t)")]
    xoB = [fpB[0:2], fpB[2:4]]
    xoD = [P.dsem("xo0"), P.dsem("xo1")]

    def load_x(i, sub):
        k = sub % 2
        r0 = i * T + sub * 128
        P.op("sp", lambda e: e.dma_start(out=xin[k], in_=x_d[r0:r0 + 128, :]), writes=xinB[k], dsem=xinD[k])

    def transpose_in(i, sub):
        k = sub % 2
        for cg in range(2):
            ps, psB = psR.next()
            for cc in range(4):
                c = cg * 4 + cc
                P.op("pe", lambda e, c=c, cc=cc, ps=ps: e.transpose(out=ps[:, cc * 128:(cc + 1) * 128], in_=xin[k][:, c * 128:(c + 1) * 128], identity=ident[:]), reads=xinB[k] + [identB], writes=[psB])
            P.op("act", lambda e, ps=ps, cg=cg: e.copy(out=xs[:, cg * 4:(cg + 1) * 4, sub * 128:(sub + 1) * 128], in_=ps[:].rearrange("p (a b) -> p a b", a=4)), reads=[psB], writes=xsB[cg * 4:(cg + 1) * 4])
            P.op("dve", lambda e, cg=cg: e.tensor_copy(out=xb[:, cg * 4:(cg + 1) * 4, sub * 128:(sub + 1) * 128], in_=xs[:, cg * 4:(cg + 1) * 4, sub * 128:(sub + 1) * 128]), reads=xsB[cg * 4:(cg + 1) * 4], writes=xbB[cg * 4:(cg + 1) * 4])

    def store_out(i):
        for sub in range(4):
            k = sub % 2
            for cg in range(2):
                ps, psB = psR.next()
                for cc in range(4):
                    c = cg * 4 + cc
                    P.op("pe", lambda e, c=c, cc=cc, ps=ps, sub=sub: e.transpose(out=ps[:, cc * 128:(cc + 1) * 128], in_=xs[:, c, sub * 128:(sub + 1) * 128], identity=ident[:]), reads=[xsB[c], identB], writes=[psB])
                if cg == 0:
                    P.op("act", lambda e, ps=ps, cg=cg, k=k: e.copy(out=xo[k][:, cg * 512:(cg + 1) * 512], in_=ps[:]), reads=[psB], writes=xoB[k])
                else:
                    P.op("dve", lambda e, ps=ps, cg=cg, k=k: e.tensor_copy(out=xo[k][:, cg * 512:(cg + 1) * 512], in_=ps[:]), reads=[psB], writes=xoB[k])
            r0 = i * T + sub * 128
            P.op("sp", lambda e, r0=r0, k=k: e.dma_start(out=y_d[r0:r0 + 128, :], in_=xo[k]), reads=xoB[k], dsem=xoD[k])

    def store_debug(i):
        store_out(i)

    for i in range(ntiles):
        for sub in range(4):
            if not (i > 0 and sub < 2 and stop is None):
                load_x(i, sub)
            transpose_in(i, sub)
        done = False
        for name, fn in (("in", lambda: None), ("ffn00", lambda: ffn(0, 0)), ("rg", rglru), ("ffn01", lambda: ffn(0, 1)), ("kv", lambda: kvproj(i)),
                         ("ffn10", lambda: ffn(1, 0)), ("attn", lambda: attention(i)), ("ffn11", lambda: ffn(1, 1))):
            fn()
            if name == "attn" and i + 1 < ntiles and stop is None:
                load_x(i + 1, 0)
                load_x(i + 1, 1)
            if stop == name:
                break
        store_out(i)
    P.fence("sp", fpB[0:4])
    P.emit()
    P.close()
    return nc


_CACHE = {}


def prep_inputs(inputs):
    f = lambda a: np.ascontiguousarray(np.asarray(a, dtype=np.float32))
    ln_g = f(inputs["ln_g"]).reshape(6, 8, 128).transpose(2, 0, 1).reshape(128, 48)
    ln_b = f(inputs["ln_b"]).reshape(6, 8, 128).transpose(2, 0, 1).reshape(128, 48)
    cw = f(inputs["rg_conv_w"])[0].reshape(4, 10, 128).transpose(2, 0, 1).reshape(128, 40)
    cb = f(inputs["rg_conv_b"])[0].reshape(10, 128).T
    gb = f(inputs["rg_gate_b"])[0].reshape(2, 10, 128).transpose(2, 0, 1).reshape(128, 20)
    lam = f(inputs["rg_lam"])[0].reshape(10, 128).T
    vecs = np.ascontiguousarray(np.concatenate([ln_g, ln_b, cw, cb, gb, lam], axis=1))
    assert vecs.shape == (128, NV)
    shared = {
        "ffn_w_in": f(inputs["ffn_w_in"]), "ffn_w_out": f(inputs["ffn_w_out"]),
        "rg_w_in": f(inputs["rg_w_in"])[0], "rg_gate_w": f(inputs["rg_gate_w"])[0], "rg_w_out": f(inputs["rg_w_out"])[0],
        "kv_w": f(inputs["kv_w"]), "attn_w_q": f(inputs["attn_w_q"])[0], "attn_w_o": f(inputs["attn_w_o"])[0],
        "vecs": vecs, "tabs": build_tables(), "ident": np.eye(128, dtype=np.float32),
    }
    return shared


def kernel(**inputs):
    x = np.ascontiguousarray(np.asarray(inputs["x"], dtype=np.float32))
    shared = prep_inputs(inputs)
    if "nc" not in _CACHE:
        _CACHE["nc"] = build_program()
    nc = _CACHE["nc"]
    in_maps = [dict(shared, x=x[b]) for b in range(8)]
    res = run_bass_kernel_spmd(nc, in_maps, core_ids=list(range(8)))
    return np.stack([np.asarray(r["y"], dtype=np.float32) for r in res.results], axis=0)
```

```python
from contextlib import ExitStack
import numpy as np
import concourse.bass as bass
import concourse.mybir as mybir

F32 = mybir.dt.float32
BF16 = mybir.dt.bfloat16
AF = mybir.ActivationFunctionType
ALU = mybir.AluOpType
AX = mybir.AxisListType

ENGS = ("pe", "act", "dve", "pool", "sp")


class DSem:
    def __init__(self, name):
        self.name = name
        self.count = 0
        self.h = None


class Op:
    __slots__ = ("eng", "idx", "fn", "waits", "signal", "dsem", "dval", "val")

    def __init__(self, eng, idx, fn, dsem):
        self.eng = eng
        self.idx = idx
        self.fn = fn
        self.waits = []
        self.signal = False
        self.dsem = dsem
        self.dval = 0
        self.val = 0


class Buf:
    def __init__(self, name):
        self.name = name
        self.last_w = None
        self.writers = {}
        self.readers = {}


class Prog:
    def __init__(self, nc):
        self.nc = nc
        self.ops = {e: [] for e in ENGS}
        self.waited = {e: {} for e in ENGS}
        self.dsems = []
        self.stack = ExitStack()

    def sbuf(self, name, shape, dt):
        return self.stack.enter_context(self.nc.sbuf_tensor("sb_" + name, list(shape), dt))

    def psum(self, name, shape, dt):
        return self.stack.enter_context(self.nc.psum_tensor("ps_" + name, list(shape), dt))

    def dsem(self, name):
        d = DSem(name)
        self.dsems.append(d)
        return d

    def _dep(self, op, other):
        if other is None or other is op or other.fn is None:
            return
        if other.dsem is None and other.eng == "pe" and op.eng == "pe" and op.dsem is None:
            return
        if other.dsem is not None:
            key, ordv = other.dsem, other.dval
        else:
            key, ordv = other.eng, other.idx
        w = self.waited[op.eng]
        if w.get(key, -1) >= ordv:
            return
        w[key] = ordv
        other.signal = True
        op.waits.append(other)

    def op(self, eng, fn, reads=(), writes=(), dsem=None):
        o = Op(eng, len(self.ops[eng]), fn, dsem)
        if dsem is not None:
            dsem.count += 16
            o.dval = dsem.count
        for b in reads:
            for r in b.writers.values():
                self._dep(o, r)
        for b in writes:
            for r in b.writers.values():
                self._dep(o, r)
            for r in b.readers.values():
                self._dep(o, r)
        key = dsem if dsem is not None else eng
        for b in reads:
            b.readers[key] = o
        for b in writes:
            b.last_w = o
            b.writers[key] = o
        self.ops[eng].append(o)
        return o

    def fence(self, eng, bufs):
        self.op(eng, None, reads=(), writes=bufs)

    def emit(self):
        nc = self.nc
        st = self.stack
        esem = {e: st.enter_context(nc.semaphore("sem_" + e)) for e in ENGS}
        for d in self.dsems:
            d.h = st.enter_context(nc.semaphore("dsem_" + d.name))
        for e in ENGS:
            c = 0
            for o in self.ops[e]:
                if o.dsem is None and o.signal:
                    c += 1
                o.val = c
        ops = self.ops

        def run(e, eng):
            for o in ops[e]:
                for y in o.waits:
                    if y.dsem is not None:
                        eng.wait_ge(y.dsem.h, y.dval)
                    else:
                        eng.wait_ge(esem[y.eng], y.val)
                if o.fn is None:
                    continue
                ins = o.fn(eng)
                if o.dsem is not None:
                    ins.then_inc(o.dsem.h, 16)
                elif o.signal:
                    ins.then_inc(esem[e], 1)

        block = st.enter_context(nc.Block())

        @block.tensor
        def _(eng):
            run("pe", eng)

        @block.scalar
        def _(eng):
            run("act", eng)

        @block.vector
        def _(eng):
            run("dve", eng)

        @block.gpsimd
        def _(eng):
            run("pool", eng)

        @block.sync
        def _(eng):
            run("sp", eng)

    def close(self):
        self.stack.close()


from concourse.bass_utils import run_bass_kernel_spmd
import os
SKIP = os.environ.get('K_SKIP', '').split(',')

T = 512
NT = 8
ALPHA = 2.0 ** 0.5
LN_EPS = 1e-5
NSLAB = 3
SLAB = 5632
NF = 10
BIGV = 1e30
NV = 176
C_LNG, C_LNB, C_CW, C_CB, C_GB, C_LAM, C_NSP8, C_NSP16, C_TMP = 0, 48, 96, 136, 146, 166, 176, 186, 196
GELU_C = 1.5957691216057308


def build_tables():
    jk = np.arange(128)[:, None]
    jq = np.arange(128)[None, :]
    tabs = np.full((128, 21, 128), BIGV, np.float32)
    order = [(4, 0), (16, 0), (4, 1), (16, 1), (16, 2), (16, 3), (16, 4)]
    for idx, (d, dl) in enumerate(order):
        m = jq - jk + 128 * dl
        if d == 4:
            valid = (m >= 0) & (m <= 128)
        else:
            valid = (m >= 0) & (m <= 512) & (m % 4 == 0)
        tabs[:, idx, :] = np.where(valid, 4 * m, BIGV)
    for dl in (0, 1):
        for de in range(-3, 4):
            dist = 4 * (jq - jk) + de + 512 * dl
            tabs[:, 7 + 7 * dl + de + 3, :] = np.where((dist >= 0) & (dist <= 128), dist, BIGV)
    return tabs.reshape(128, 21 * 128)


class Rot:
    def __init__(self, items):
        self.items = list(items)
        self.i = 0

    def next(self):
        it = self.items[self.i % len(self.items)]
        self.i += 1
        return it


def build_program(ntiles=NT, stop=None):
    nc = bass.Bass("TRN2", target_bir_lowering=False)
    P = Prog(nc)

    def din(name, shape):
        return nc.dram_tensor(name, list(shape), F32, kind="ExternalInput").ap()

    x_d = din("x", [4096, 1024])
    y_d = nc.dram_tensor("y", [4096, 1024], F32, kind="ExternalOutput").ap()
    w_ffn_in = din("ffn_w_in", [2, 2, 1024, 5632])
    w_ffn_out = din("ffn_w_out", [2, 2, 2816, 1024])
    w_rg_in = din("rg_w_in", [1024, 2560])
    w_gate = din("rg_gate_w", [2, 5, 256, 256])
    w_rg_out = din("rg_w_out", [1280, 1024])
    w_kv = din("kv_w", [1024, 2048])
    w_q = din("attn_w_q", [1024, 3072])
    w_o = din("attn_w_o", [1024, 1024])
    vecs_d = din("vecs", [128, NV])
    tabs_d = din("tabs", [128, 21 * 128])
    ident_d = din("ident", [128, 128])

    slabs = [P.sbuf(f"slab{i}", [128, SLAB], BF16) for i in range(NSLAB)]
    slabB = [Buf(f"slab{i}") for i in range(NSLAB)]
    slabD = [P.dsem(f"slab{i}") for i in range(NSLAB)]
    KT = P.sbuf("KT", [128, 8, 5 * T], BF16)
    KTB = [[Buf(f"KT{c}_{s}") for s in range(5)] for c in range(8)]
    V = P.sbuf("V", [128, 320, 65], BF16)
    VB = [[[Buf(f"V{s}_{r}_{hv}") for hv in range(2)] for r in range(4)] for s in range(5)]
    tabs = P.sbuf("tabs", [128, 21, 128], F32)
    tabsB = Buf("tabs")
    vecs = P.sbuf("vecs", [128, 208], F32)
    vecsB = Buf("vecs")
    ident = P.sbuf("ident", [128, 128], F32)
    identB = Buf("ident")
    identb = P.sbuf("identb", [128, 128], BF16)
    onesb = P.sbuf("onesb", [128, 128], BF16)
    cst = P.sbuf("cst", [128, 4], F32)
    cstB = Buf("cst")
    xs = P.sbuf("xs", [128, 8, T], F32)
    xsB = [Buf(f"xs{c}") for c in range(8)]
    xb = P.sbuf("xb", [128, 8, T], BF16)
    xbB = [Buf(f"xb{c}") for c in range(8)]
    big = P.sbuf("big", [128, 22, T], BF16)
    bigB = [Buf(f"big{c}") for c in range(22)]
    fp = P.sbuf("fp", [128, NF, T], F32)
    fpB = [Buf(f"fp{c}") for c in range(NF)]
    ubr = [P.sbuf(f"ubr{i}", [128, 3 + T], F32) for i in range(2)]
    ubrB = [Buf(f"ubr{i}") for i in range(2)]
    Eb = [P.sbuf(f"E{i}", [128, 512], BF16) for i in range(4)]
    EB = [Buf(f"E{i}") for i in range(4)]
    hprev = P.sbuf("hprev", [128, 10], F32)
    hprevB = [Buf(f"hprev{c}") for c in range(10)]
    halo = P.sbuf("halo", [128, 10, 3], F32)
    haloB = [Buf(f"halo{c}") for c in range(10)]
    rden = P.sbuf("rden", [128, 8], F32)
    rdenB = [Buf(f"rden{c}") for c in range(8)]
    pb = [P.psum(f"pb{i}", [128, 512], F32) for i in range(4)]
    pbB = [Buf(f"pb{i}") for i in range(4)]
    po = [P.psum(f"po{i}", [128, 512], F32) for i in range(4)]
    poB = [[Buf(f"po{i}")] for i in range(4)]
    psR5 = Rot(list(zip(pb, pbB)))
    psR = Rot(list(zip(pb, pbB)) + [(po[k], poB[k][0]) for k in range(2)])

    def fp_rot(idx):
        return Rot([(fp[:, k, :], fpB[k]) for k in idx])

    d_tabs, d_vecs, d_ident = P.dsem("tabs"), P.dsem("vecs"), P.dsem("ident")
    P.op("sp", lambda e: e.dma_start(out=tabs[:], in_=tabs_d.rearrange("p (a b) -> p a b", a=21)), writes=[tabsB], dsem=d_tabs)
    P.op("sp", lambda e: e.dma_start(out=vecs[:, 0:NV], in_=vecs_d), writes=[vecsB], dsem=d_vecs)
    P.op("sp", lambda e: e.dma_start(out=ident[:], in_=ident_d), writes=[identB], dsem=d_ident)
    P.op("dve", lambda e: e.memset(onesb[:], 1.0 / 1024.0), writes=[identB])
    P.op("dve", lambda e: e.tensor_copy(out=identb[:], in_=ident[:]), reads=[identB], writes=[identB])
    P.op("dve", lambda e: e.memset(cst[:, 0:1], LN_EPS), writes=[cstB])
    P.op("dve", lambda e: e.memset(cst[:, 1:2], 4.0 * LN_EPS), writes=[cstB])
    P.op("dve", lambda e: e.memset(cst[:, 2:3], 1.0), writes=[cstB])
    P.op("dve", lambda e: e.memset(hprev[:], 0.0), writes=hprevB)
    P.op("dve", lambda e: e.memset(halo[:], 0.0), writes=haloB)
    allVB = [b for s in VB for r in s for b in r]
    if 'vmem' not in SKIP:
        P.op("dve", lambda e: e.memset(V[:, :, 64:65], 1.0), writes=allVB)
    if 'lam' not in SKIP:
        P.op("act", lambda e: e.activation(out=vecs[:, C_TMP:C_TMP + 10], in_=vecs[:, C_LAM:C_LAM + 10], func=AF.Exp, scale=-1.0), reads=[vecsB], writes=[vecsB])
        P.op("act", lambda e: e.activation(out=vecs[:, C_TMP:C_TMP + 10], in_=vecs[:, C_TMP:C_TMP + 10], func=AF.Ln, bias=cst[:, 2:3], scale=1.0), reads=[vecsB, cstB], writes=[vecsB])
    P.op("dve", lambda e: e.tensor_scalar(out=vecs[:, C_NSP8:C_NSP8 + 10], in0=vecs[:, C_TMP:C_TMP + 10], scalar1=-8.0, scalar2=None, op0=ALU.mult), reads=[vecsB], writes=[vecsB])
    P.op("dve", lambda e: e.tensor_scalar(out=vecs[:, C_NSP16:C_NSP16 + 10], in0=vecs[:, C_TMP:C_TMP + 10], scalar1=-16.0, scalar2=None, op0=ALU.mult), reads=[vecsB], writes=[vecsB])

    slab_ctr = [0]

    def load_slab(parts):
        k = slab_ctr[0] % NSLAB
        slab_ctr[0] += 1
        t = slabs[k]
        first = True
        for ov, ia in parts:
            o_ap = ov(t)
            if first:
                P.op("pool", lambda e, o=o_ap, i=ia: e.dma_start(out=o, in_=i), writes=[slabB[k]], dsem=slabD[k])
                first = False
            else:
                o = P.op("pool", lambda e, o=o_ap, i=ia: e.dma_start(out=o, in_=i), dsem=slabD[k])
                slabB[k].last_w = o
                slabB[k].writers[slabD[k]] = o
        return t, slabB[k]

    def mm(out_ap, lhsT, rhs, start, stop, reads, writes):
        P.op("pe", lambda e: e.matmul(out_ap, lhsT=lhsT, rhs=rhs, start=start, stop=stop), reads=reads, writes=writes)

    def sq_view(m):
        return fp[:, m // 2, :].bitcast(BF16)[:, (m % 2) * T:(m % 2 + 1) * T]

    def ln_chunk(m):
        P.op("act", lambda e: e.activation(out=sq_view(m), in_=xs[:, m, :], func=AF.Square), reads=[xsB[m]], writes=[fpB[m // 2]])
        P.op("dve", lambda e: e.tensor_copy(out=xb[:, m, :], in_=xs[:, m, :]), reads=[xsB[m]], writes=[xbB[m]])

    def ln_stats(m):
        mm(po[2][:], onesb[:], xb[:, m, :], m == 0, m == 7, [identB, xbB[m]], [poB[2][0]])
        mm(po[3][:], onesb[:], sq_view(m), m == 0, m == 7, [identB, fpB[m // 2]], [poB[3][0]])

    def ln_finish(ln_idx, eps_col):
        pm, pmB, pq, pqB = po[2], poB[2][0], po[3], poB[3][0]
        mean, meanB = fp[:, 4, :], fpB[4]
        var, varB = fp[:, 5, :], fpB[5]
        P.op("act", lambda e: e.copy(out=mean, in_=pm[:]), reads=[pmB], writes=[meanB])
        P.op("dve", lambda e: e.tensor_tensor(out=var, in0=mean, in1=mean, op=ALU.mult), reads=[meanB], writes=[varB])
        P.op("dve", lambda e: e.tensor_tensor(out=var, in0=pq[:], in1=var, op=ALU.subtract), reads=[pqB, varB], writes=[varB])
        P.op("act", lambda e: e.activation(out=var, in_=var, func=AF.Sqrt, bias=cst[:, eps_col:eps_col + 1], scale=1.0), reads=[varB, cstB], writes=[varB])
        P.op("dve", lambda e: e.reciprocal(out=var, in_=var), reads=[varB], writes=[varB])
        for c in range(8):
            g = vecs[:, C_LNG + ln_idx * 8 + c:C_LNG + ln_idx * 8 + c + 1]
            bb = vecs[:, C_LNB + ln_idx * 8 + c:C_LNB + ln_idx * 8 + c + 1]
            P.op("dve", lambda e, c=c: e.tensor_tensor(out=xs[:, c, :], in0=xs[:, c, :], in1=mean, op=ALU.subtract), reads=[meanB, xsB[c]], writes=[xsB[c]])
            P.op("dve", lambda e, c=c: e.tensor_tensor(out=xs[:, c, :], in0=xs[:, c, :], in1=var, op=ALU.mult), reads=[varB, xsB[c]], writes=[xsB[c]])
            P.op("act", lambda e, c=c, g=g, bb=bb: e.activation(out=xb[:, c, :], in_=xs[:, c, :], func=AF.Identity, bias=bb, scale=g), reads=[xsB[c], vecsB], writes=[xbB[c]])
            P.op("act", lambda e, c=c, g=g, bb=bb: e.activation(out=xs[:, c, :], in_=xs[:, c, :], func=AF.Identity, bias=bb, scale=g), reads=[xsB[c], vecsB], writes=[xsB[c]])

    def residual_ln(chunk_mm, scalar, ln_idx, eps_col):
        for m in range(8):
            ps, psB = chunk_mm(m)
            P.op("dve", lambda e, m=m, ps=ps: e.scalar_tensor_tensor(out=xs[:, m, :], in0=xs[:, m, :], scalar=scalar, in1=ps[:], op0=ALU.mult, op1=ALU.add), reads=[xsB[m], psB], writes=[xsB[m]])
            ln_chunk(m)
            if m > 0:
                ln_stats(m - 1)
        ln_stats(7)
        ln_finish(ln_idx, eps_col)

    def mm_wave(accs, nk, rhs_fn, rhsB_fn):
        for kc in range(nk):
            for ps_ap, psB, lf, rd in accs:
                mm(ps_ap, lf(kc), rhs_fn(kc), kc == 0, kc == nk - 1, rd + [rhsB_fn(kc)], [psB])

    def ffn(l, i2):
        win = w_ffn_in[l, i2].rearrange("(kc p) (h f) -> p kc h f", p=128, h=2)
        wout = w_ffn_out[l, i2].rearrange("(kc p) n -> p kc n", p=128)
        fr = fp_rot([0, 1, 2])
        for s in range(11):
            def ov(t, h):
                return t[:, 0:4096].rearrange("p (kc h f) -> p kc h f", kc=8, h=2)[:, :, h, :]
            sl, slB = load_slab([(lambda t: ov(t, 0), win[:, :, 0, s * 256:(s + 1) * 256]),
                                 (lambda t: ov(t, 1), win[:, :, 1, s * 256:(s + 1) * 256])])
            sv = sl[:, 0:4096].rearrange("p (kc h f) -> p kc h f", kc=8, h=2)
            accs = []
            for jj in range(2):
                pg, pgB = psR.next()
                pu, puB = psR.next()
                accs.append((pg[:], pgB, (lambda kc, jj=jj, sv=sv: sv[:, kc, 0, jj * 128:(jj + 1) * 128]), [slB]))
                accs.append((pu[:], puB, (lambda kc, jj=jj, sv=sv: sv[:, kc, 1, jj * 128:(jj + 1) * 128]), [slB]))
            if s == 0:
                mm_wave(accs, 8, lambda kc: xb[:, kc, :], lambda kc: xbB[kc])
            else:
                mm_wave(accs[0:2], 8, lambda kc: xb[:, kc, :], lambda kc: xbB[kc])
                mm_wave(accs[2:4], 8, lambda kc: xb[:, kc, :], lambda kc: xbB[kc])
            for jj in range(2):
                j = 2 * s + jj
                pg_ap, pgB = accs[2 * jj][0], accs[2 * jj][1]
                pu_ap, puB = accs[2 * jj + 1][0], accs[2 * jj + 1][1]
                sg, sgB = fr.next()
                P.op("act", lambda e, sg=sg, pg_ap=pg_ap: e.activation(out=sg, in_=pg_ap, func=AF.Silu), reads=[pgB], writes=[sgB])
                P.op("dve", lambda e, sg=sg, pu_ap=pu_ap, j=j: e.tensor_tensor(out=big[:, j, :], in0=sg, in1=pu_ap, op=ALU.mult), reads=[sgB, puB], writes=[bigB[j]])
        slabs_out = {}

        def chunk_mm(m):
            s, mmi = m // 2, m % 2
            if mmi == 0:
                def ov2(t, a, bb):
                    return t[:, 0:5632].rearrange("p (kc f) -> p kc f", kc=22)[:, a:bb, :]
                slabs_out[s] = load_slab([(lambda t: ov2(t, 0, 11), wout[:, 0:11, s * 256:(s + 1) * 256]),
                                          (lambda t: ov2(t, 11, 22), wout[:, 11:22, s * 256:(s + 1) * 256])])
            sl, slB = slabs_out[s]
            sv = sl[:, 0:5632].rearrange("p (kc f) -> p kc f", kc=22)
            ps, psB = psR.next()
            for kc in range(22):
                mm(ps[:], sv[:, kc, mmi * 128:(mmi + 1) * 128], big[:, kc, :], kc == 0, kc == 21, [slB, bigB[kc]], [psB])
            return ps, psB

        residual_ln(chunk_mm, 2.0 * ALPHA, l * 3 + (0 if i2 == 0 else 2), 1)

    def rglru():
        win = w_rg_in.rearrange("(kc p) (h f) -> p kc h f", p=128, h=2)
        fr = fp_rot(list(range(NF)))
        ubR = Rot(list(zip(ubr, ubrB)))
        yh = big
        for n in range(5):
            def ov(t, h):
                return t[:, 0:4096].rearrange("p (kc h f) -> p kc h f", kc=8, h=2)[:, :, h, :]

            def ovg(t, g):
                return t[:, 4096 + g * 512:4096 + (g + 1) * 512].rearrange("p (a d) -> p a d", a=2)
            sl, slB = load_slab([(lambda t: ov(t, 0), win[:, :, 0, n * 256:(n + 1) * 256]),
                                 (lambda t: ov(t, 1), win[:, :, 1, n * 256:(n + 1) * 256]),
                                 (lambda t: ovg(t, 0), w_gate[0, n].rearrange("(kc p) d -> p kc d", p=128)),
                                 (lambda t: ovg(t, 1), w_gate[1, n].rearrange("(kc p) d -> p kc d", p=128))])
            sv = sl[:, 0:4096].rearrange("p (kc h f) -> p kc h f", kc=8, h=2)
            gv = sl[:, 4096:5120].rearrange("p (a d) -> p a d", a=4)
            gslB = slB
            pys, pus, accs = [], [], []
            for q in range(2):
                py, pyB = psR.next()
                pu, puB = psR.next()
                pys.append((py, pyB))
                pus.append((pu, puB))
                accs.append((py[:], pyB, (lambda kc, q=q, sv=sv: sv[:, kc, 0, q * 128:(q + 1) * 128]), [slB]))
                accs.append((pu[:], puB, (lambda kc, q=q, sv=sv: sv[:, kc, 1, q * 128:(q + 1) * 128]), [slB]))
            mm_wave(accs, 8, lambda kc: xb[:, kc, :], lambda kc: xbB[kc])
            ys = [fr.next() for q in range(2)]
            us = [fr.next() for q in range(2)]
            ubs = [ubR.next() for q in range(2)]
            Q2 = (0, 1)
            chs = [2 * n, 2 * n + 1]
            for q in Q2:
                P.op("act", lambda e, y=ys[q][0], py=pys[q][0]: e.activation(out=y, in_=py[:], func=AF.Square), reads=[pys[q][1]], writes=[ys[q][1]])
            for q in Q2:
                P.op("act", lambda e, ub_=ubs[q][0], pu=pus[q][0]: e.copy(out=ub_[:, 3:3 + T], in_=pu[:]), reads=[pus[q][1]], writes=[ubs[q][1]])
                P.op("act", lambda e, ub_=ubs[q][0], ch=chs[q]: e.copy(out=ub_[:, 0:3], in_=halo[:, ch, :]), reads=[haloB[chs[q]]], writes=[ubs[q][1]])
            for q in Q2:
                P.op("dve", lambda e, y=ys[q][0]: e.tensor_scalar(out=y, in0=y, scalar1=0.044715, scalar2=1.0, op0=ALU.mult, op1=ALU.add), reads=[ys[q][1]], writes=[ys[q][1]])
            for q in Q2:
                P.op("dve", lambda e, y=ys[q][0], py=pys[q][0]: e.tensor_tensor(out=y, in0=y, in1=py[:], op=ALU.mult), reads=[ys[q][1], pys[q][1]], writes=[ys[q][1]])
            for q in Q2:
                P.op("act", lambda e, y=ys[q][0]: e.activation(out=y, in_=y, func=AF.Sigmoid, scale=GELU_C), reads=[ys[q][1]], writes=[ys[q][1]])
            cw = lambda w_, ch: vecs[:, C_CW + w_ * 10 + ch:C_CW + w_ * 10 + ch + 1]
            for q in Q2:
                P.op("dve", lambda e, u=us[q][0], ub_=ubs[q][0], ch=chs[q]: e.tensor_scalar(out=u, in0=ub_[:, 3:3 + T], scalar1=cw(3, ch), scalar2=vecs[:, C_CB + ch:C_CB + ch + 1], op0=ALU.mult, op1=ALU.add), reads=[ubs[q][1], vecsB], writes=[us[q][1]])
            for w_ in range(3):
                for q in Q2:
                    P.op("dve", lambda e, u=us[q][0], ub_=ubs[q][0], ch=chs[q], w_=w_: e.scalar_tensor_tensor(out=u, in0=ub_[:, w_:w_ + T], scalar=cw(w_, ch), in1=u, op0=ALU.mult, op1=ALU.add), reads=[ubs[q][1], us[q][1], vecsB], writes=[us[q][1]])
            for q in Q2:
                P.op("act", lambda e, u=us[q][0], q=q: e.copy(out=big[:, 20 + q, :], in_=u), reads=[us[q][1]], writes=[bigB[20 + q]])
                P.op("act", lambda e, ub_=ubs[q][0], ch=chs[q]: e.copy(out=halo[:, ch, :], in_=ub_[:, T:T + 3]), reads=[ubs[q][1]], writes=[haloB[chs[q]]])
            for q in Q2:
                P.op("dve", lambda e, y=ys[q][0], py=pys[q][0]: e.tensor_tensor(out=y, in0=y, in1=py[:], op=ALU.mult), reads=[ys[q][1], pys[q][1]], writes=[ys[q][1]])
            pgs = {}
            for g in range(2):
                for q in Q2:
                    ps, psB = psR.next()
                    for k2 in range(2):
                        mm(ps[:], gv[:, g * 2 + k2, q * 128:(q + 1) * 128], big[:, 20 + k2, :], k2 == 0, k2 == 1, [gslB, bigB[20 + k2]], [psB])
                    pgs[(g, q)] = (ps, psB)
            As = [fr.next() for q in Q2]
            Bs = [fr.next() for q in Q2]
            Cs = [fr.next() for q in Q2]
            gb = lambda g, ch: vecs[:, C_GB + g * 10 + ch:C_GB + g * 10 + ch + 1]
            for q in Q2:
                P.op("act", lambda e, a=As[q][0], ps=pgs[(0, q)][0], ch=chs[q]: e.activation(out=a, in_=ps[:], func=AF.Sigmoid, bias=gb(0, ch), scale=1.0), reads=[pgs[(0, q)][1], vecsB], writes=[As[q][1]])
            for q in Q2:
                P.op("act", lambda e, c3=Cs[q][0], ps=pgs[(1, q)][0], ch=chs[q]: e.activation(out=c3, in_=ps[:], func=AF.Sigmoid, bias=gb(1, ch), scale=1.0), reads=[pgs[(1, q)][1], vecsB], writes=[Cs[q][1]])
            for q in Q2:
                P.op("act", lambda e, a=As[q][0], b2=Bs[q][0], ch=chs[q]: e.activation(out=b2, in_=a, func=AF.Exp, scale=vecs[:, C_NSP16 + ch:C_NSP16 + ch + 1]), reads=[As[q][1], vecsB], writes=[Bs[q][1]])
            for q in Q2:
                P.op("act", lambda e, a=As[q][0], ch=chs[q]: e.activation(out=a, in_=a, func=AF.Exp, scale=vecs[:, C_NSP8 + ch:C_NSP8 + ch + 1]), reads=[As[q][1], vecsB], writes=[As[q][1]])
            for q in Q2:
                P.op("dve", lambda e, c3=Cs[q][0], u=us[q][0]: e.tensor_tensor(out=c3, in0=c3, in1=u, op=ALU.mult), reads=[Cs[q][1], us[q][1]], writes=[Cs[q][1]])
            for q in Q2:
                P.op("act", lambda e, b2=Bs[q][0]: e.activation(out=b2, in_=b2, func=AF.Sqrt, bias=cst[:, 2:3], scale=-1.0), reads=[Bs[q][1], cstB], writes=[Bs[q][1]])
            for q in Q2:
                P.op("dve", lambda e, b2=Bs[q][0], c3=Cs[q][0]: e.tensor_tensor(out=b2, in0=b2, in1=c3, op=ALU.mult), reads=[Bs[q][1], Cs[q][1]], writes=[Bs[q][1]])
            for q in Q2:
                P.op("dve", lambda e, a=As[q][0], b2=Bs[q][0], c3=Cs[q][0], ch=chs[q]: e.tensor_tensor_scan(out=c3, data0=a, data1=b2, initial=hprev[:, ch:ch + 1], op0=ALU.mult, op1=ALU.add), reads=[As[q][1], Bs[q][1], hprevB[chs[q]]], writes=[Cs[q][1]])
            for q in Q2:
                P.op("act", lambda e, c3=Cs[q][0], ch=chs[q]: e.copy(out=hprev[:, ch:ch + 1], in_=c3[:, T - 1:T]), reads=[Cs[q][1]], writes=[hprevB[chs[q]]])
            for q in Q2:
                P.op("dve", lambda e, y=ys[q][0], c3=Cs[q][0], ch=chs[q]: e.tensor_tensor(out=yh[:, ch, :], in0=y, in1=c3, op=ALU.mult), reads=[ys[q][1], Cs[q][1]], writes=[bigB[chs[q]]])
        wout = w_rg_out.rearrange("(kc p) n -> p kc n", p=128)
        slabs_out = {}

        def chunk_mm(m):
            s, mmi = m // 4, m % 4
            if mmi == 0:
                slabs_out[s] = load_slab([(lambda t: t[:, 0:5120].rearrange("p (kc f) -> p kc f", kc=10), wout[:, :, s * 512:(s + 1) * 512])])
            sl, slB = slabs_out[s]
            sv = sl[:, 0:5120].rearrange("p (kc f) -> p kc f", kc=10)
            ps, psB = psR.next()
            for kc in range(10):
                mm(ps[:], sv[:, kc, mmi * 128:(mmi + 1) * 128], yh[:, kc, :], kc == 0, kc == 9, [slB, bigB[kc]], [psB])
            return ps, psB

        residual_ln(chunk_mm, ALPHA, 1, 0)

    def kvproj(i):
        slot = i % 5
        wk = w_kv.rearrange("(kc p) n -> p kc n", p=128)
        for s in range(2):
            sl, slB = load_slab([(lambda t: t[:, 0:4096].rearrange("p (kc f) -> p kc f", kc=8), wk[:, :, s * 512:(s + 1) * 512])])
            sv = sl[:, 0:4096].rearrange("p (kc f) -> p kc f", kc=8)
            accs = []
            for mmi in range(4):
                ps, psB = psR.next()
                accs.append((ps[:], psB, (lambda kc, mmi=mmi, sv=sv: sv[:, kc, mmi * 128:(mmi + 1) * 128]), [slB]))
            mm_wave(accs, 8, lambda kc: xb[:, kc, :], lambda kc: xbB[kc])
            for mmi in range(4):
                c = 4 * s + mmi
                ps_ap, psB = accs[mmi][0], accs[mmi][1]
                if mmi % 2 == 0:
                    P.op("act", lambda e, c=c, ps_ap=ps_ap: e.copy(out=KT[:, c, slot * T:(slot + 1) * T].rearrange("p (r j) -> p r j", r=4), in_=ps_ap.rearrange("p (j r) -> p r j", r=4)), reads=[psB], writes=[KTB[c][slot]])
                else:
                    P.op("dve", lambda e, c=c, ps_ap=ps_ap: e.tensor_copy(out=KT[:, c, slot * T:(slot + 1) * T].rearrange("p (r j) -> p r j", r=4), in_=ps_ap.rearrange("p (j r) -> p r j", r=4)), reads=[psB], writes=[KTB[c][slot]])
        for hv in range(2):
            sl, slB = load_slab([(lambda t: t[:, 0:4096].rearrange("p (kc f) -> p kc f", kc=8), wk[:, :, 1024 + hv * 512:1024 + (hv + 1) * 512])])
            sv = sl[:, 0:4096].rearrange("p (kc f) -> p kc f", kc=8)
            for r4 in range(4):
                ps, psB = psR.next()
                for kc in range(8):
                    mm(ps[:], xb[:, kc, r4:T:4], sv[:, kc, :], kc == 0, kc == 7, [slB, xbB[kc]], [psB])
                base = slot * 64 + r4 * 16 + hv * 8
                P.op("dve", lambda e, ps=ps, base=base: e.tensor_copy(out=V[:, base:base + 8, 0:64], in_=ps[:].rearrange("p (h d) -> p h d", h=8)), reads=[psB], writes=[VB[slot][r4][hv]])

    def attention(i):
        wq = w_q.rearrange("(kc p) (g f) -> p kc g f", p=128, g=3)
        QTz = big[:, 16:22, :].rearrange("p (h g) t -> p h g t", h=2)
        qtB = bigB[16:22]
        P.op("dve", lambda e: e.memset(QTz[64:128, 0, :, :], 0.0), writes=qtB)
        P.op("dve", lambda e: e.memset(QTz[0:64, 1, :, :], 0.0), writes=qtB)
        AO = big[:, 8:16, :].rearrange("p a t -> p (a t)").rearrange("p (r f) -> p r f", r=4)
        AT = big[:, 0:8, :]
        smR = fp_rot([0, 1, 2, 3, 4, 5])
        ER = Rot(list(zip(Eb, EB)))
        gen = [(1, 0), (2, 0), (1, 1), (2, 1), (2, 2), (2, 3), (2, 4)]
        nvalid = [2, 4, 5, 6, 7][min(i, 4)]
        def qproj(c):
            def ov(t, g):
                return t[:, 0:3072].rearrange("p (kc g f) -> p kc g f", kc=8, g=3)[:, :, g, :]
            sl, slB = load_slab([(lambda t, g=g: ov(t, g), wq[:, :, g, c * 128:(c + 1) * 128]) for g in range(3)])
            sv = sl[:, 0:3072].rearrange("p (kc g f) -> p kc g f", kc=8, g=3)
            accs = []
            for g in range(3):
                ps, psB = psR5.next()
                accs.append((ps[:], psB, (lambda kc, g=g, sv=sv: sv[:, kc, g, :]), [slB]))
            mm_wave(accs, 8, lambda kc: xb[:, kc, :], lambda kc: xbB[kc])
            for g in range(3):
                ps, psB = accs[g][0], accs[g][1]
                wr = [qtB[g], qtB[3 + g]]
                P.op("act", lambda e, ps=ps, g=g: e.copy(out=QTz[0:64, 0, g, :].rearrange("p (r j) -> p r j", r=4), in_=ps[0:64, :].rearrange("p (j r) -> p r j", r=4)), reads=[psB], writes=wr)
                P.op("dve", lambda e, ps=ps, g=g: e.tensor_copy(out=QTz[64:128, 1, g, :].rearrange("p (r j) -> p r j", r=4), in_=ps[64:128, :].rearrange("p (j r) -> p r j", r=4)), reads=[psB], writes=wr)

        for c in range(8):
            qproj(c)
            jobs = []
            for hh in range(2):
                d1 = [(dl, rk) for dl in (1, 0) if i - dl >= 0 for rk in (3, 2, 1, 0)]
                for k_, (dl, rk) in enumerate(d1):
                    jobs.append(("d1", hh, dl, rk, k_ == 0))
                for r4 in range(4):
                    nb_a = 2 if i == 0 else 4
                    jobs.append(("genA", hh, r4, nb_a, nvalid <= 4))
                    if nvalid > 4:
                        jobs.append(("genB", hh, r4, nvalid - 4, True))
            state = {}

            def emit_qk(jn):
                job = jobs[jn]
                kind, hh = job[0], job[1]
                h = 2 * c + hh
                pbase = 64 * hh
                slope8 = -8.0 * (2.0 ** (-(h + 1) / 2.0))
                ps, psB = psR5.next()
                if kind == "d1":
                    dl, rk = job[2], job[3]
                    slot = (i - dl) % 5
                    mm(ps[:], KT[:, c, slot * T + rk * 128:slot * T + (rk + 1) * 128], QTz[:, hh, 0, :], True, True, [KTB[c][slot], qtB[0], qtB[3]], [psB])
                    ncol = 512
                    t0 = 10 + 7 * dl - rk
                    nb = 4
                elif kind == "genA":
                    r4, nb = job[2], job[3]
                    for dl in range(nb // 2):
                        slot = (i - dl) % 5
                        mm(ps[:, dl * 256:(dl + 1) * 256], KT[:, c, slot * T + r4 * 128:slot * T + (r4 + 1) * 128], QTz[:, hh, 1:3, r4 * 128:(r4 + 1) * 128], True, True, [KTB[c][slot], qtB[1], qtB[2], qtB[4], qtB[5]], [psB])
                    ncol = nb * 128
                    t0 = 0
                else:
                    r4, nb = job[2], job[3]
                    for bi in range(nb):
                        slot = (i - (2 + bi)) % 5
                        mm(ps[:, bi * 128:(bi + 1) * 128], KT[:, c, slot * T + r4 * 128:slot * T + (r4 + 1) * 128], QTz[:, hh, 2, r4 * 128:(r4 + 1) * 128], True, True, [KTB[c][slot], qtB[2], qtB[5]], [psB])
                    ncol = nb * 128
                    t0 = 4
                sm, smB = smR.next()
                P.op("dve", lambda e, sm=sm, ps=ps, nb=nb, t0=t0, ncol=ncol, slope8=slope8: e.scalar_tensor_tensor(out=sm[:, 0:ncol], in0=tabs[:, t0:t0 + nb, :].rearrange("p a b -> p (a b)"), scalar=slope8, in1=ps[:, 0:ncol], op0=ALU.mult, op1=ALU.add), reads=[tabsB, psB], writes=[smB])
                E, EBuf = ER.next()
                P.op("act", lambda e, E=E, sm=sm, ncol=ncol: e.activation(out=E[:, 0:ncol], in_=sm[:, 0:ncol], func=AF.Exp, scale=0.125), reads=[smB], writes=[EBuf])
                state[jn] = (E, EBuf)

            def emit_pv(jn):
                job = jobs[jn]
                kind, hh = job[0], job[1]
                h = 2 * c + hh
                E, EBuf = state.pop(jn)
                if kind == "d1":
                    dl, rk, first = job[2], job[3], job[4]
                    slot = (i - dl) % 5
                    vix = slot * 64 + rk * 16 + h
                    for r4 in range(4):
                        mm(po[r4][:, 0:65], E[:, r4 * 128:(r4 + 1) * 128], V[:, vix, :], first, False, [EBuf, VB[slot][rk][h // 8]], [poB[r4][0]])
                    return
                r4, nb, last = job[2], job[3], job[4]
                acc = po[r4][:, 0:65]
                accB = poB[r4][0]
                for bi in range(nb):
                    if kind == "genA":
                        dl = bi // 2
                    else:
                        dl = 2 + bi
                    slot = (i - dl) % 5
                    vix = slot * 64 + r4 * 16 + h
                    mm(acc, E[:, bi * 128:(bi + 1) * 128], V[:, vix, :], False, last and bi == nb - 1, [EBuf, VB[slot][r4][h // 8]], [accB])
                if last:
                    rd = rden[:, hh * 4 + r4:hh * 4 + r4 + 1]
                    rdB = rdenB[hh * 4 + r4]
                    P.op("dve", lambda e, rd=rd, acc=acc: e.reciprocal(out=rd, in_=acc[:, 64:65]), reads=[accB], writes=[rdB])
                    P.op("act", lambda e, rd=rd, acc=acc, r4=r4, h=h: e.activation(out=AO[:, r4, h * 64:(h + 1) * 64], in_=acc[:, 0:64], func=AF.Identity, scale=rd), reads=[accB, rdB], writes=[bigB[8 + 2 * r4], bigB[9 + 2 * r4]])

            LOOK = 3
            for jn in range(len(jobs) + LOOK):
                if jn < len(jobs):
                    emit_qk(jn)
                if jn - LOOK >= 0:
                    emit_pv(jn - LOOK)
            ptf, pTB = psR5.next()
            pT = ptf[:].bitcast(BF16)
            for r4 in range(4):
                P.op("pe", lambda e, r4=r4, c=c, pT=pT: e.transpose(out=pT[:, r4 * 128:(r4 + 1) * 128], in_=AO[:, r4, c * 128:(c + 1) * 128], identity=identb[:]), reads=[bigB[8 + 2 * r4], bigB[9 + 2 * r4], identB], writes=[pTB])
            P.op("dve", lambda e, c=c, pT=pT: e.tensor_copy(out=AT[:, c, :].rearrange("p (j r) -> p r j", r=4), in_=pT[:, 0:512].rearrange("p (r j) -> p r j", r=4)), reads=[pTB], writes=[bigB[c]])
        wo = w_o.rearrange("(kc p) n -> p kc n", p=128)
        slabs_out = {}

        def chunk_mm(m):
            s, mmi = m // 4, m % 4
            if mmi == 0:
                slabs_out[s] = load_slab([(lambda t: t[:, 0:4096].rearrange("p (kc f) -> p kc f", kc=8), wo[:, :, s * 512:(s + 1) * 512])])
            sl, slB = slabs_out[s]
            sv = sl[:, 0:4096].rearrange("p (kc f) -> p kc f", kc=8)
            ps, psB = psR.next()
            for kc in range(8):
                mm(ps[:], sv[:, kc, mmi * 128:(mmi + 1) * 128], AT[:, kc, :], kc == 0, kc == 7, [slB, bigB[kc]], [psB])
            return ps, psB

        residual_ln(chunk_mm, ALPHA, 4, 0)

    xin = [fp[:, 6:8, :].rearrange("p a t -> p (a t)"), fp[:, 8:10, :].rearrange("p a t -> p (a t)")]
    xinB = [fpB[6:8], fpB[8:10]]
    xinD = [P.dsem("xin0"), P.dsem("xin1")]
    xo = [fp[:, 0:2, :].rearrange("p a t -> p (a t)"), fp[:, 2:4, :].rearrange("p a t -> p (a t)")]
    xoB = [fpB[0:2], fpB[2:4]]
    xoD = [P.dsem("xo0"), P.dsem("xo1")]

    def load_x(i, sub):
        k = sub % 2
        r0 = i * T + sub * 128
        P.op("sp", lambda e: e.dma_start(out=xin[k], in_=x_d[r0:r0 + 128, :]), writes=xinB[k], dsem=xinD[k])

    def transpose_in(i, sub):
        k = sub % 2
        for cg in range(2):
            ps, psB = psR.next()
            for cc in range(4):
                c = cg * 4 + cc
                P.op("pe", lambda e, c=c, cc=cc, ps=ps: e.transpose(out=ps[:, cc * 128:(cc + 1) * 128], in_=xin[k][:, c * 128:(c + 1) * 128], identity=ident[:]), reads=xinB[k] + [identB], writes=[psB])
            P.op("act", lambda e, ps=ps, cg=cg: e.copy(out=xs[:, cg * 4:(cg + 1) * 4, sub * 128:(sub + 1) * 128], in_=ps[:].rearrange("p (a b) -> p a b", a=4)), reads=[psB], writes=xsB[cg * 4:(cg + 1) * 4])
            P.op("dve", lambda e, cg=cg: e.tensor_copy(out=xb[:, cg * 4:(cg + 1) * 4, sub * 128:(sub + 1) * 128], in_=xs[:, cg * 4:(cg + 1) * 4, sub * 128:(sub + 1) * 128]), reads=xsB[cg * 4:(cg + 1) * 4], writes=xbB[cg * 4:(cg + 1) * 4])

    def store_out(i):
        for sub in range(4):
            k = sub % 2
            for cg in range(2):
                ps, psB = psR.next()
                for cc in range(4):
                    c = cg * 4 + cc
                    P.op("pe", lambda e, c=c, cc=cc, ps=ps, sub=sub: e.transpose(out=ps[:, cc * 128:(cc + 1) * 128], in_=xs[:, c, sub * 128:(sub + 1) * 128], identity=ident[:]), reads=[xsB[c], identB], writes=[psB])
                if cg == 0:
                    P.op("act", lambda e, ps=ps, cg=cg, k=k: e.copy(out=xo[k][:, cg * 512:(cg + 1) * 512], in_=ps[:]), reads=[psB], writes=xoB[k])
                else:
                    P.op("dve", lambda e, ps=ps, cg=cg, k=k: e.tensor_copy(out=xo[k][:, cg * 512:(cg + 1) * 512], in_=ps[:]), reads=[psB], writes=xoB[k])
            r0 = i * T + sub * 128
            P.op("sp", lambda e, r0=r0, k=k: e.dma_start(out=y_d[r0:r0 + 128, :], in_=xo[k]), reads=xoB[k], dsem=xoD[k])

    def store_debug(i):
        store_out(i)

    for i in range(ntiles):
        for sub in range(4):
            if not (i > 0 and sub < 2 and stop is None):
                load_x(i, sub)
            transpose_in(i, sub)
        done = False
        for name, fn in (("in", lambda: None), ("ffn00", lambda: ffn(0, 0)), ("rg", rglru), ("ffn01", lambda: ffn(0, 1)), ("kv", lambda: kvproj(i)),
                         ("ffn10", lambda: ffn(1, 0)), ("attn", lambda: attention(i)), ("ffn11", lambda: ffn(1, 1))):
            fn()
            if name == "attn" and i + 1 < ntiles and stop is None:
                load_x(i + 1, 0)
                load_x(i + 1, 1)
            if stop == name:
                break
        store_out(i)
    P.fence("sp", fpB[0:4])
    P.emit()
    P.close()
    return nc


_CACHE = {}


def prep_inputs(inputs):
    f = lambda a: np.ascontiguousarray(np.asarray(a, dtype=np.float32))
    ln_g = f(inputs["ln_g"]).reshape(6, 8, 128).transpose(2, 0, 1).reshape(128, 48)
    ln_b = f(inputs["ln_b"]).reshape(6, 8, 128).transpose(2, 0, 1).reshape(128, 48)
    cw = f(inputs["rg_conv_w"])[0].reshape(4, 10, 128).transpose(2, 0, 1).reshape(128, 40)
    cb = f(inputs["rg_conv_b"])[0].reshape(10, 128).T
    gb = f(inputs["rg_gate_b"])[0].reshape(2, 10, 128).transpose(2, 0, 1).reshape(128, 20)
    lam = f(inputs["rg_lam"])[0].reshape(10, 128).T
    vecs = np.ascontiguousarray(np.concatenate([ln_g, ln_b, cw, cb, gb, lam], axis=1))
    assert vecs.shape == (128, NV)
    shared = {
        "ffn_w_in": f(inputs["ffn_w_in"]), "ffn_w_out": f(inputs["ffn_w_out"]),
        "rg_w_in": f(inputs["rg_w_in"])[0], "rg_gate_w": f(inputs["rg_gate_w"])[0], "rg_w_out": f(inputs["rg_w_out"])[0],
        "kv_w": f(inputs["kv_w"]), "attn_w_q": f(inputs["attn_w_q"])[0], "attn_w_o": f(inputs["attn_w_o"])[0],
        "vecs": vecs, "tabs": build_tables(), "ident": np.eye(128, dtype=np.float32),
    }
    return shared


def kernel(**inputs):
    x = np.ascontiguousarray(np.asarray(inputs["x"], dtype=np.float32))
    shared = prep_inputs(inputs)
    if "nc" not in _CACHE:
        _CACHE["nc"] = build_program()
    nc = _CACHE["nc"]
    in_maps = [dict(shared, x=x[b]) for b in range(8)]
    res = run_bass_kernel_spmd(nc, in_maps, core_ids=list(range(8)))
    return np.stack([np.asarray(r["y"], dtype=np.float32) for r in res.results], axis=0)
```
